# Optimizing a Trainium2 kernel written in Bass

```python
import functools
import jax, jax.numpy as jnp
from jax import lax
import numpy as np

D_MODEL = 1024
BATCH = 8
SEQ = 8192
DEPTH = 1
DEC_BATCH = 32
DEC_SEQ = 16
PAST_LEN = 2048

CHUNK = 64
LEFT_CHUNKS = 8
ATT_WINDOW = LEFT_CHUNKS * CHUNK
BAND = (LEFT_CHUNKS + 1) * CHUNK
N_HEADS = 8
HEAD_DIM = 64
D_ATT = N_HEADS * HEAD_DIM
D_CONV = D_MODEL // 2
CONV_WIDTH = 3
MAX_REL = 128
D_FF = 2816
D_PLE = 256
N_NORMS = 7
EPS = 1e-6
NEG_INF = -1e30
SPLITS = (D_ATT, 2 * D_ATT, 3 * D_ATT, 3 * D_ATT + D_CONV, 3 * D_ATT + 2 * D_CONV,
          3 * D_ATT + 3 * D_CONV, 3 * D_ATT + 3 * D_CONV + D_MODEL)
D_IN = 3 * D_ATT + 3 * D_CONV + 2 * D_MODEL

kernel_name = "streaming_hybrid_chunkattn_shortconv_step"


def rmsnorm(x, g):
    xf = x.astype(jnp.float32)
    y = xf * lax.rsqrt(jnp.mean(xf * xf, axis=-1, keepdims=True) + EPS) * g.astype(jnp.float32)
    return y.astype(x.dtype)


def swiglu(x, w_gate, w_up, w_down):
    return (jax.nn.silu(x @ w_gate) * (x @ w_up)) @ w_down


def rel_bias_lookup(table, d):
    idx = jnp.clip(d, -MAX_REL, MAX_REL) + MAX_REL
    return table[:, idx].astype(jnp.float32)


def chunk_band_attention(q, k, v, table):
    S = q.shape[1]
    nc = S // CHUNK
    n = jnp.arange(BAND)
    i = jnp.arange(CHUNK)
    bias = rel_bias_lookup(table, i[:, None] + ATT_WINDOW - n[None, :])
    c = jnp.arange(nc)
    valid = (c[:, None] * CHUNK - ATT_WINDOW + n[None, :]) >= 0
    scale = HEAD_DIM ** -0.5
    pad = ((ATT_WINDOW, 0), (0, 0), (0, 0))

    def one_seq(args):
        qs, ks, vs = args
        kp = jnp.pad(ks, pad).reshape(nc + LEFT_CHUNKS, CHUNK, N_HEADS, HEAD_DIM)
        vp = jnp.pad(vs, pad).reshape(nc + LEFT_CHUNKS, CHUNK, N_HEADS, HEAD_DIM)
        kb = jnp.concatenate([kp[j:j + nc] for j in range(LEFT_CHUNKS + 1)], axis=1)
        vb = jnp.concatenate([vp[j:j + nc] for j in range(LEFT_CHUNKS + 1)], axis=1)
        qc = qs.reshape(nc, CHUNK, N_HEADS, HEAD_DIM)
        s = jnp.einsum('cqhd,ckhd->chqk', qc, kb).astype(jnp.float32) * scale + bias[None]
        s = jnp.where(valid[:, None, None, :], s, NEG_INF)
        p = jax.nn.softmax(s, axis=-1).astype(vs.dtype)
        o = jnp.einsum('chqk,ckhd->cqhd', p, vb)
        return o.reshape(S, D_ATT)

    return lax.map(one_seq, (q, k, v))


def cached_band_attention(q, k, v, k_cache, v_cache, table):
    Lc = k_cache.shape[1]
    T = q.shape[1]
    kk = jnp.concatenate([k_cache.astype(k.dtype), k], axis=1)
    vv = jnp.concatenate([v_cache.astype(v.dtype), v], axis=1)
    d = (Lc + jnp.arange(T))[:, None] - jnp.arange(Lc + T)[None, :]
    bias = rel_bias_lookup(table, d)
    s = jnp.einsum('bqhd,bkhd->bhqk', q, kk).astype(jnp.float32) * (HEAD_DIM ** -0.5) + bias[None]
    p = jax.nn.softmax(s, axis=-1).astype(vv.dtype)
    o = jnp.einsum('bhqk,bkhd->bqhd', p, vv)
    return o.reshape(q.shape[0], T, D_ATT)


def short_conv(s, prefix, w):
    T = s.shape[1]
    sf = jnp.concatenate([prefix.astype(s.dtype), s], axis=1)
    y = sum(sf[:, j:j + T] * w[j] for j in range(CONV_WIDTH))
    return y, sf[:, -(CONV_WIDTH - 1):]


def layer_forward(h, p_l, conv_prefix, attend, norm_g, w1_gate, w1_up, w1_down, w_in, conv_w,
                  w_att_out, w_conv_out, w_out, w2_gate, w2_up, w2_down, w_ple_gate, w_ple_proj):
    Bn, T, _ = h.shape
    h = h + 0.5 * rmsnorm(swiglu(rmsnorm(h, norm_g[0]), w1_gate, w1_up, w1_down), norm_g[1])
    u = rmsnorm(h, norm_g[2])
    z = u @ w_in
    q, k, v, x_in, b_gate, c_gate, g_att, g_conv = jnp.split(z, SPLITS, axis=-1)
    q = q.reshape(Bn, T, N_HEADS, HEAD_DIM)
    k = k.reshape(Bn, T, N_HEADS, HEAD_DIM)
    v = v.reshape(Bn, T, N_HEADS, HEAD_DIM)
    o_att = attend(q, k, v)
    conv_y, conv_state = short_conv(c_gate * x_in, conv_prefix, conv_w)
    y_conv = b_gate * conv_y
    m = jax.nn.sigmoid(g_att) * (o_att @ w_att_out) + jax.nn.sigmoid(g_conv) * (y_conv @ w_conv_out)
    h = h + rmsnorm(m @ w_out, norm_g[3])
    h = h + 0.5 * rmsnorm(swiglu(rmsnorm(h, norm_g[4]), w2_gate, w2_up, w2_down), norm_g[5])
    h = h + rmsnorm(jax.nn.sigmoid(h @ w_ple_gate) * (p_l @ w_ple_proj), norm_g[6])
    return h, k, v, conv_state


def setup_inputs(seed: int = 0) -> dict:
    key = jax.random.key(seed)
    ks = jax.random.split(key, 24)
    f32 = jnp.float32
    att_cache = min(ATT_WINDOW, PAST_LEN)

    def w(k, shape, fan_in):
        return jax.random.normal(k, shape, f32) * (fan_in ** -0.5)

    return {
        "x_prompt": jax.random.normal(ks[0], (BATCH, SEQ, D_MODEL), f32),
        "x_sample": jax.random.normal(ks[1], (DEC_BATCH, DEC_SEQ, D_MODEL), f32),
        "cache_k": jax.random.normal(ks[2], (DEPTH, DEC_BATCH, att_cache, N_HEADS, HEAD_DIM), f32),
        "cache_v": jax.random.normal(ks[3], (DEPTH, DEC_BATCH, att_cache, N_HEADS, HEAD_DIM), f32),
        "cache_conv": jax.random.normal(ks[4], (DEPTH, DEC_BATCH, CONV_WIDTH - 1, D_CONV), f32),
        "p_prompt": jax.random.normal(ks[5], (DEPTH, BATCH, SEQ, D_PLE), f32),
        "p_sample": jax.random.normal(ks[6], (DEPTH, DEC_BATCH, DEC_SEQ, D_PLE), f32),
        "norm_g": 1.0 + 0.05 * jax.random.normal(ks[7], (DEPTH, N_NORMS, D_MODEL), f32),
        "w1_gate": w(ks[8], (DEPTH, D_MODEL, D_FF), D_MODEL),
        "w1_up": w(ks[9], (DEPTH, D_MODEL, D_FF), D_MODEL),
        "w1_down": w(ks[10], (DEPTH, D_FF, D_MODEL), D_FF),
        "w_in": w(ks[11], (DEPTH, D_MODEL, D_IN), D_MODEL),
        "conv_w": w(ks[12], (DEPTH, CONV_WIDTH, D_CONV), CONV_WIDTH),
        "rel_bias": 0.2 * jax.random.normal(ks[13], (DEPTH, N_HEADS, 2 * MAX_REL + 1), f32),
        "w_att_out": w(ks[14], (DEPTH, D_ATT, D_MODEL), D_ATT),
        "w_conv_out": w(ks[15], (DEPTH, D_CONV, D_MODEL), D_CONV),
        "w_out": w(ks[16], (DEPTH, D_MODEL, D_MODEL), D_MODEL),
        "w2_gate": w(ks[17], (DEPTH, D_MODEL, D_FF), D_MODEL),
        "w2_up": w(ks[18], (DEPTH, D_MODEL, D_FF), D_MODEL),
        "w2_down": w(ks[19], (DEPTH, D_FF, D_MODEL), D_FF),
        "w_ple_gate": w(ks[20], (DEPTH, D_MODEL, D_MODEL), D_MODEL),
        "w_ple_proj": w(ks[21], (DEPTH, D_PLE, D_MODEL), D_PLE),
    }


def reference(x_prompt, x_sample, cache_k, cache_v, cache_conv, p_prompt, p_sample, norm_g,
              w1_gate, w1_up, w1_down, w_in, conv_w, rel_bias, w_att_out, w_conv_out, w_out,
              w2_gate, w2_up, w2_down, w_ple_gate, w_ple_proj):
    hp, hs = x_prompt, x_sample
    kp_l, vp_l, cp_l, ks_l, vs_l, cs_l = [], [], [], [], [], []
    for l in range(DEPTH):
        weights = (norm_g[l], w1_gate[l], w1_up[l], w1_down[l], w_in[l], conv_w[l],
                   w_att_out[l], w_conv_out[l], w_out[l], w2_gate[l], w2_up[l], w2_down[l],
                   w_ple_gate[l], w_ple_proj[l])
        attend_p = functools.partial(chunk_band_attention, table=rel_bias[l])
        prefix_p = jnp.zeros((hp.shape[0], CONV_WIDTH - 1, D_CONV), hp.dtype)
        hp, k_p, v_p, c_p = layer_forward(hp, p_prompt[l], prefix_p, attend_p, *weights)
        keep = min(ATT_WINDOW, k_p.shape[1])
        kp_l.append(k_p[:, -keep:])
        vp_l.append(v_p[:, -keep:])
        cp_l.append(c_p)
        attend_s = functools.partial(cached_band_attention, k_cache=cache_k[l], v_cache=cache_v[l],
                                     table=rel_bias[l])
        hs, k_s, v_s, c_s = layer_forward(hs, p_sample[l], cache_conv[l], attend_s, *weights)
        ks_l.append(k_s)
        vs_l.append(v_s)
        cs_l.append(c_s)
    return (hp, hs, jnp.stack(kp_l), jnp.stack(vp_l), jnp.stack(cp_l),
            jnp.stack(ks_l), jnp.stack(vs_l), jnp.stack(cs_l))
```

```python
import numpy as np
import concourse.bass as bass
import concourse.mybir as mybir
from concourse.bass_utils import run_bass_kernel_spmd
from contextlib import ExitStack

F32 = mybir.dt.float32
BF16 = mybir.dt.bfloat16
AF = mybir.ActivationFunctionType
ALU = mybir.AluOpType

NCORES = 8
D = 1024
DC = 8
DFF = 2816
FC = 22
T = 512
TS = 64
SEQ = 8192
NCV = 8
NSLOT = 9
SLAB = 16
EPS = 1e-6
NEG = -30000.0


def _block_list():
    bl = []

    def ffn(tag):
        for j in range(FC):
            for kc in range(DC):
                bl.append((tag + "_gate", kc, j))
            for kc in range(DC):
                bl.append((tag + "_up", kc, j))
        for m in range(DC):
            for j in range(FC):
                bl.append((tag + "_down", j, m))

    ffn("w1")
    for m in range(8):
        for kc in range(DC):
            bl.append(("w_in", kc, m))
    for kc in range(DC):
        for m in range(8, 12):
            bl.append(("w_in", kc, m))
    for m in range(12, 24):
        for kc in range(DC):
            bl.append(("w_in", kc, m))
    for m in range(DC):
        for kc in range(4):
            bl.append(("w_att_out", kc, m))
        for kc in range(4):
            bl.append(("w_conv_out", kc, m))
        for kc in range(DC):
            bl.append(("w_in", kc, 24 + m))
        for kc in range(DC):
            bl.append(("w_in", kc, 32 + m))
    for m in range(DC):
        for kc in range(DC):
            bl.append(("w_out", kc, m))
    ffn("w2")
    for m in range(DC):
        for kc in range(DC):
            bl.append(("w_ple_gate", kc, m))
        for kc in range(2):
            bl.append(("w_ple_proj", kc, m))
    return bl


BLOCKS = _block_list()
NBLK = len(BLOCKS)
assert NBLK % SLAB == 0
NSLAB = NBLK // SLAB


class Tracker:
    CE = ("pe", "act", "dve", "pool")

    def __init__(self):
        self.prog = {e: [] for e in ("pe", "act", "dve", "pool", "sp")}
        self.cnt = {e: 0 for e in self.CE}
        self.dangling = {e: False for e in self.CE}
        self.known = {e: {} for e in self.prog}
        self.last_w = {}
        self.readers = {}
        self.dcnt = {}

    def _deps(self, eng, reads, writes):
        need = {}

        def add(tok):
            if tok is None:
                return
            s, v = tok
            if need.get(s, 0) < v:
                need[s] = v

        for r in reads:
            add(self.last_w.get(r))
            if isinstance(r, tuple) and r[0] == "ps":
                for s, v in self.readers.get(r, {}).items():
                    if s != eng:
                        add((s, v))
        for w in writes:
            add(self.last_w.get(w))
            for s, v in self.readers.get(w, {}).items():
                add((s, v))
        kn = self.known[eng]
        for s, v in need.items():
            if eng == "pe" and s == "pe":
                continue
            if kn.get(s, 0) >= v:
                continue
            kn[s] = v
            self.prog[eng].append(("wait", s, v))

    def _commit(self, tok, reads, writes):
        for r in reads:
            d = self.readers.setdefault(r, {})
            if d.get(tok[0], 0) < tok[1]:
                d[tok[0]] = tok[1]
        for w in writes:
            self.last_w[w] = tok
            self.readers[w] = {}

    def op(self, eng, fn, reads=(), writes=(), inc=True):
        self._deps(eng, reads, writes)
        if inc:
            self.cnt[eng] += 1
            tok = (eng, self.cnt[eng])
            self.dangling[eng] = False
            self.prog[eng].append(("op", fn, eng, 1))
        else:
            tok = (eng, self.cnt[eng] + 1)
            self.dangling[eng] = True
            self.prog[eng].append(("op", fn, None, 0))
        self._commit(tok, reads, writes)

    def dma(self, queue, fn, sem, reads=(), writes=(), n=1):
        self._deps(queue, reads, writes)
        self.dcnt[sem] = self.dcnt.get(sem, 0) + 16 * n
        tok = (sem, self.dcnt[sem])
        self.prog[queue].append(("dma", fn, sem, n))
        self._commit(tok, reads, writes)

    def final_wait(self, queue, sems):
        for s in sems:
            if s in self.dcnt:
                self.prog[queue].append(("wait", s, self.dcnt[s]))


class _Stop(Exception):
    pass


def build(NT=16, with_sample=True, stop=None):
    nc = bass.Bass("TRN2", target_bir_lowering=False)
    SP = NT * T
    HW = T // 2
    xT = nc.dram_tensor("xT", [D, SP], F32, kind="ExternalInput").ap()
    pT = nc.dram_tensor("pT", [256, SP], F32, kind="ExternalInput").ap()
    xsT = nc.dram_tensor("xsT", [D, TS], F32, kind="ExternalInput").ap()
    psT = nc.dram_tensor("psT", [256, TS], F32, kind="ExternalInput").ap()
    wall = nc.dram_tensor("wall", [NSLAB, 128, SLAB * 128], F32, kind="ExternalInput").ap()
    wbf = nc.dram_tensor("wbf", [NSLAB, 128, SLAB * 128], BF16, kind="Internal").ap()
    gvd = nc.dram_tensor("gv", [128, 56], F32, kind="ExternalInput").ap()
    cwd = nc.dram_tensor("cw", [128, 12], F32, kind="ExternalInput").ap()
    biasd = nc.dram_tensor("biasp", [128, 5, 1024], F32, kind="ExternalInput").ap()
    biassd = nc.dram_tensor("biass", [128, 4, 128], F32, kind="ExternalInput").ap()
    biasnd = nc.dram_tensor("biasn", [16, 128], F32, kind="ExternalInput").ap()
    identd = nc.dram_tensor("ident", [128, 128], F32, kind="ExternalInput").ap()
    ckTd = nc.dram_tensor("ckT", [4, 128, 4, 512], F32, kind="ExternalInput").ap()
    cvd = nc.dram_tensor("cv", [4, 512, 512], F32, kind="ExternalInput").ap()
    ccTd = nc.dram_tensor("ccT", [128, 4, 4, 2], F32, kind="ExternalInput").ap()

    yT = nc.dram_tensor("yT", [D, SP], F32, kind="ExternalOutput").ap()
    ysT = nc.dram_tensor("ysT", [D, TS], F32, kind="ExternalOutput").ap()
    kpT = nc.dram_tensor("kpT", [512, 512], F32, kind="ExternalOutput").ap()
    vp = nc.dram_tensor("vp", [512, 512], F32, kind="ExternalOutput").ap()
    cpT = nc.dram_tensor("cpT", [512, 2], F32, kind="ExternalOutput").ap()
    ksT = nc.dram_tensor("ksT", [512, TS], F32, kind="ExternalOutput").ap()
    vs = nc.dram_tensor("vs", [4, 16, 512], F32, kind="ExternalOutput").ap()
    csT = nc.dram_tensor("csT", [512, 4, 2], F32, kind="ExternalOutput").ap()

    tr = Tracker()
    es = ExitStack()
    with es:
        def sb(name, shape, dt):
            return es.enter_context(nc.sbuf_tensor(name, shape, dt))

        hbuf = [sb("h0", [128, DC, T], F32), sb("h1", [128, DC, T], F32)]
        ub = sb("ub", [128, DC, T], BF16)
        mTb = sb("mT", [128, DC, T], BF16)
        sq = sb("sq", [128, DC, T], BF16)
        ysb = sb("ysb", [128, DC, T], F32)
        hid = sb("hid", [128, FC, T], BF16)
        NTB = 10
        tb = sb("tbuf", [128, NTB, HW], F32)
        rstd = sb("rstd", [128, T], F32)
        qT = sb("qT", [128, 4, T], BF16)
        kT = sb("kT", [128, 4, 2 * T], BF16)
        Vr = sb("Vr", [128, 8, 512], BF16)
        sS = sb("sS", [128, 4, T + 2], F32)
        sSs = sb("sSs", [128, 4, 4, 18], F32)
        ycT = sb("ycT", [128, 4, T], BF16)
        PT = sb("PT", [128, 2, 1024], BF16)
        PTs = sb("PTs", [128, 5, 128], BF16)
        Rr = sb("Rr", [128, 512], F32)
        oT = sb("oT", [128, 4, T], BF16)
        pTb = [sb("pTb0", [128, 2, T], BF16), sb("pTb1", [128, 2, T], BF16)]
        biasT = sb("biasT", [128, 5, 1024], BF16)
        biasS = sb("biasS", [128, 4, 128], BF16)
        biasN = sb("biasN", [16, 128], BF16)
        ident = sb("identb", [128, 128], BF16)
        onesn = sb("onesn", [128, 128], BF16)
        ones1 = sb("ones1", [128, 64], BF16)
        epsb = sb("epsb", [128, 1], F32)
        gv = sb("gvs", [128, 56], F32)
        gh = sb("ghs", [128, 56], F32)
        cw = sb("cws", [128, 12], F32)
        vnew = sb("vnew", [16, 4, 512], BF16)
        wr = sb("wring", [128, NSLOT, SLAB * 128], BF16)
        ps = es.enter_context(nc.psum_tensor("ps", [128, 8, 512], F32))

        OUT_SEMS = ["oh0", "oh1", "o_cp", "o_cs"] + ["ot%d" % i for i in range(NTB)]
        sem_names = ["pe", "act", "dve", "pool", "hs0", "hs1", "ps0", "ps1", "cst", "cst2", "cinit", "ch0", "ch1"] + \
                    OUT_SEMS + ["w%d" % i for i in range(NSLOT)] + ["wb%d" % i for i in range(NSLOT)] + ["wq%d" % i for i in range(NSLOT)]
        sems = {n: es.enter_context(nc.semaphore("s_" + n)) for n in sem_names}

        bank_rot = [0, 1, 2, 3, 4, 7]
        st = {"bank": 0, "tb": 0, "slab_loaded": 0}
        n_pass = NT + (1 if with_sample else 0)
        total_slabs = n_pass * NSLAB

        def nbank():
            b = bank_rot[st["bank"] % len(bank_rot)]
            st["bank"] += 1
            return b

        def ntb():
            i = st["tb"] % NTB
            st["tb"] += 1
            return i

        def load_slab(g):
            slot = g % NSLOT
            dst = wr[:, slot, :]
            if g < NSLAB:
                src = wall[g]
                tr.dma("pool", lambda e, dst=dst, src=src: [e.dma_start(out=dst, in_=src)],
                       "wq%d" % slot, reads=(), writes=(("w", slot),))
                tr.dma("sp", lambda e, dst=dst, g=g: [e.dma_start(out=wbf[g], in_=dst)],
                       "wb%d" % slot, reads=(("w", slot),), writes=(("wbf", g),))
            else:
                src = wbf[g % NSLAB]
                tr.dma("sp", lambda e, dst=dst, src=src: [e.dma_start(out=dst, in_=src)],
                       "w%d" % slot, reads=(("wbf", g % NSLAB),), writes=(("w", slot),))

        def convert_weights():
            for g in range(NSLAB):
                rd = (("wbf", g - NCV),) if g >= NCV else ()
                tr.dma("pool", lambda e, g=g: [e.dma_start(out=wbf[g], in_=wall[g])],
                       "cv%d" % (g % NCV), reads=rd, writes=(("wbf", g),))

        def ensure_slabs(upto):
            while st["slab_loaded"] <= upto and st["slab_loaded"] < total_slabs:
                load_slab(st["slab_loaded"])
                st["slab_loaded"] += 1

        def wget(gb, expect=None):
            if expect is not None:
                assert BLOCKS[gb % NBLK] == expect, (BLOCKS[gb % NBLK], expect)
            g = gb // SLAB
            assert g < st["slab_loaded"], ("slab not issued", g, st["slab_loaded"])
            slot = g % NSLOT
            off = (gb % SLAB) * 128
            return wr[:, slot, off:off + 128], ("w", slot)

        def wget4(gb, expect0):
            assert gb % 4 == 0 and BLOCKS[gb % NBLK] == expect0
            g = gb // SLAB
            assert g < st["slab_loaded"]
            slot = g % NSLOT
            off = (gb % SLAB) * 128
            return wr[:, slot, off:off + 512], ("w", slot)

        def mm_group(out_ap, out_res, terms):
            n = len(terms)
            for i, (l, lr, r, rr) in enumerate(terms):
                tr.op("pe",
                      lambda e, l=l, r=r, i=i: e.matmul(out_ap, lhsT=l, rhs=r, start=(i == 0),
                                                        stop=(i == n - 1)),
                      reads=(lr, rr), writes=(out_res,), inc=(i == n - 1))

        def act(out, in_, func, reads, writes, scale=1.0, bias=None):
            if bias is None:
                tr.op("act", lambda e: e.activation(out=out, in_=in_, func=func, scale=scale),
                      reads=reads, writes=writes)
            else:
                tr.op("act", lambda e: e.activation(out=out, in_=in_, func=func, scale=scale,
                                                    bias=bias), reads=reads, writes=writes)

        def dve(fn, reads, writes):
            tr.op("dve", fn, reads=reads, writes=writes)

        def out_dma(fn, reads, sem, n=1):
            tr.dma("pool", fn, sem, reads=reads, writes=(), n=n)

        tr.dma("sp", lambda e: [e.dma_start(out=gv[:], in_=gvd), e.dma_start(out=cw[:], in_=cwd)],
               "cst", writes=("gv", "cw"), n=2)
        tr.dma("pool", lambda e: [e.dma_start(out=biasT[:, i, :], in_=biasd[:, i, :]) for i in range(5)]
               + [e.dma_start(out=biasS[:], in_=biassd), e.dma_start(out=biasN[:], in_=biasnd),
                  e.dma_start(out=ident[:], in_=identd)],
               "cinit", writes=("biasT", "biasS", "biasN", "ident"), n=8)
        dve(lambda e: e.tensor_scalar(out=gh[:], in0=gv[:], scalar1=0.5, scalar2=None, op0=ALU.mult),
            reads=("gv",), writes=("gh",))
        dve(lambda e: e.memset(onesn[:], 1.0 / 1024.0), reads=(), writes=("onesn",))
        dve(lambda e: e.memset(ones1[:], 1.0), reads=(), writes=("ones1",))
        dve(lambda e: e.memset(epsb[:], EPS), reads=(), writes=("epsb",))
        dve(lambda e: e.memset(sS[:, :, 0:2], 0.0), reads=(), writes=("sS",))
        dve(lambda e: e.memset(
            biasT[0:64, 0, :].rearrange("p (g q) -> p g q", q=128)[:, :, 64:128], NEG),
            reads=(), writes=("biasT",))
        dve(lambda e: e.memset(
            biasT[64:128, 4, :].rearrange("p (g q) -> p g q", q=128)[:, :, 0:64], NEG),
            reads=(), writes=("biasT",))

        class Half:
            def __init__(self, i, hf, sample=False):
                self.i, self.hf, self.sample = i, hf, sample
                self.W = TS if sample else HW
                self.c0 = 0 if sample else hf * HW
                self.cs = slice(self.c0, self.c0 + self.W)
                self.h = hbuf[i % 2]
                self.hres = "h%d" % (i % 2)
                self.pb = pTb[i % 2]
                self.pres = "p%d" % (i % 2)
                self.gb0 = i * NBLK
                self.last = (not sample) and (i == NT - 1)

            def R(self, name, *idx):
                return (name, self.hf) + idx

            def H(self, c):
                return (self.hres, self.hf, c)

        def stats_all(hv):
            bn = nbank()
            mm_group(ps[:, bn, 0:hv.W], ("ps", bn),
                     [(onesn[:], "onesn", sq[:, c, hv.cs], hv.R("sq", c)) for c in range(DC)])
            ti = ntb()
            act(tb[:, ti, 0:hv.W], ps[:, bn, 0:hv.W], AF.Ln, reads=(("ps", bn), "epsb"),
                writes=(("tb", ti),), bias=epsb[:, 0:1])
            act(rstd[:, hv.cs], tb[:, ti, 0:hv.W], AF.Exp, reads=(("tb", ti),), writes=(hv.R("rstd"),),
                scale=-0.5)

        def u_pre_sq(gi):
            def fn(c):
                def f(hv, gb):
                    act(sq[:, c, hv.cs], hv.h[:, c, hv.cs], AF.Square, reads=(hv.H(c),),
                        writes=(hv.R("sq", c),))
                return f
            return [(0, 0.4, fn(c)) for c in range(DC)]

        def u_op(hs, hv, gi, c, to_m):
            dst, dres = (mTb, "mT") if to_m else (ub, "ub")
            dve(lambda e: e.scalar_tensor_tensor(
                out=dst[:, c, hv.cs], in0=hs.h[:, c, hv.cs], scalar=gv[:, gi * 8 + c:gi * 8 + c + 1],
                in1=rstd[:, hv.cs], op0=ALU.mult, op1=ALU.mult),
                reads=(hs.H(c), "gv", hv.R("rstd")), writes=(hv.R(dres, c),))

        def u_pre_fin(gi, to_m=False):
            def fin(hv, gb):
                stats_all(hv)

            def fn(c):
                def f(hv, gb):
                    u_op(hv, hv, gi, c, to_m)
                return f
            return [(0, 2.3, fin)] + [(0, 0.45, fn(c)) for c in range(DC)]

        def u_post(gi, then_sq, next_sq=False, then_hb=False, g_in_chain=None):
            def first(hv, gb):
                stats_all(hv)

            def fn(c):
                def f(hv, gb):
                    if g_in_chain is None:
                        dve(lambda e: e.tensor_tensor(out=ysb[:, c, hv.cs], in0=ysb[:, c, hv.cs],
                                                      in1=rstd[:, hv.cs], op=ALU.mult),
                            reads=(hv.R("ysb", c), hv.R("rstd")), writes=(hv.R("ysb", c),))
                    else:
                        gsrc, gres, gix = g_in_chain
                        dve(lambda e: e.scalar_tensor_tensor(
                            out=ysb[:, c, hv.cs], in0=ysb[:, c, hv.cs], scalar=gsrc[:, gix * 8 + c:gix * 8 + c + 1],
                            in1=rstd[:, hv.cs], op0=ALU.mult, op1=ALU.mult),
                            reads=(hv.R("ysb", c), hv.R("rstd"), gres), writes=(hv.R("ysb", c),))
                    dve(lambda e: e.tensor_tensor(out=hv.h[:, c, hv.cs], in0=hv.h[:, c, hv.cs],
                                                  in1=ysb[:, c, hv.cs], op=ALU.add),
                        reads=(hv.R("ysb", c), hv.H(c)), writes=(hv.H(c),))
                    if then_sq:
                        act(sq[:, c, hv.cs], hv.h[:, c, hv.cs], AF.Square, reads=(hv.H(c),),
                            writes=(hv.R("sq", c),))
                    if then_hb:
                        act(ub[:, c, hv.cs], hv.h[:, c, hv.cs], AF.Copy, reads=(hv.H(c),),
                            writes=(hv.R("ub", c),))
                    if next_sq and hv.next is not None:
                        hn = hv.next
                        act(sq[:, c, hv.cs], hn.h[:, c, hv.cs], AF.Square, reads=(hn.H(c),),
                            writes=(hv.R("sq", c),))
                return f

            def nfin(hv, gb):
                if hv.next is not None:
                    stats_all(hv)
            us = [(0, 2.5, first)] + [(0, 0.9, fn(c)) for c in range(DC)]
            if next_sq:
                us.append((0, 2.3, nfin))
            return us

        def evac_sq(hv, bn, m, gi, half, doubled=False):
            assert not (half and doubled)
            g_ = gh if (half or doubled) else gv
            act(ysb[:, m, hv.cs], ps[:, bn, 0:hv.W], AF.Identity,
                reads=(("ps", bn), "gh" if (half or doubled) else "gv"), writes=(hv.R("ysb", m),),
                scale=g_[:, gi * 8 + m:gi * 8 + m + 1])
            act(sq[:, m, hv.cs], ps[:, bn, 0:hv.W], AF.Square, reads=(("ps", bn),), writes=(hv.R("sq", m),),
                scale=(0.5 if doubled else 1.0))

        def u_ffn(tag, gpost, from_m=False, with_next=False):
            us = []
            src, sres = (mTb, "mT") if from_m else (ub, "ub")

            def gu(j):
                def f(hv, gb):
                    W = hv.W
                    bn = nbank()
                    terms = []
                    for kc in range(DC):
                        w, wres = wget(gb + kc, (tag + "_gate", kc, j))
                        terms.append((w, wres, src[:, kc, hv.cs], hv.R(sres, kc)))
                    mm_group(ps[:, bn, 0:W], ("ps", bn), terms)
                    terms = []
                    for kc in range(DC):
                        w, wres = wget(gb + 8 + kc, (tag + "_up", kc, j))
                        terms.append((w, wres, src[:, kc, hv.cs], hv.R(sres, kc)))
                    mm_group(ps[:, bn, 256:256 + W], ("ps", bn), terms)
                    ti = ntb()
                    act(tb[:, ti, 0:W], ps[:, bn, 0:W], AF.Silu, reads=(("ps", bn),), writes=(("tb", ti),))
                    dve(lambda e: e.tensor_tensor(out=hid[:, j, hv.cs], in0=tb[:, ti, 0:W],
                                                  in1=ps[:, bn, 256:256 + W], op=ALU.mult),
                        reads=(("tb", ti), ("ps", bn)), writes=(hv.R("hid", j),))
                return f

            def down(m):
                def f(hv, gb):
                    bn = nbank()
                    terms = []
                    for j in range(FC):
                        w, wres = wget(gb + j, (tag + "_down", j, m))
                        terms.append((w, wres, hid[:, j, hv.cs], hv.R("hid", j)))
                    mm_group(ps[:, bn, 0:hv.W], ("ps", bn), terms)
                    evac_sq(hv, bn, m, gpost, True)
                return f
            nxt = []
            if with_next:
                def nsq(c):
                    def f(hv, gb):
                        if hv.next is not None:
                            hn = hv.next
                            act(sq[:, c, hv.cs], hn.h[:, c, hv.cs], AF.Square, reads=(hn.H(c),),
                                writes=(hv.R("sq", c),))
                    return f

                def nfin(hv, gb):
                    if hv.next is not None:
                        stats_all(hv)

                def nun(c):
                    def f(hv, gb):
                        if hv.next is not None:
                            u_op(hv.next, hv, 0, c, True)
                    return f
                nxt = [(0, 0.3, nsq(c)) for c in range(DC)] + [(0, 1.0, nfin)] + \
                      [(0, 0.45, nun(c)) for c in range(DC)]
            for j in range(FC):
                us.append((16, None, gu(j)))
                if j >= 2 and nxt:
                    us.append(nxt.pop(0))
            assert not nxt
            for m in range(DC):
                us.append((22, None, down(m)))
            return us

        def dense8(hv, gb, m):
            bn = nbank()
            terms = []
            for kc in range(DC):
                w, wres = wget(gb + kc, ("w_in", kc, m))
                terms.append((w, wres, ub[:, kc, hv.cs], hv.R("ub", kc)))
            mm_group(ps[:, bn, 0:hv.W], ("ps", bn), terms)
            return bn

        def u_inproj_prompt():
            us = []

            def q(m):
                def f(hv, gb):
                    bn = dense8(hv, gb, m)
                    act(qT[:, m, hv.cs], ps[:, bn, 0:hv.W], AF.Copy, reads=(("ps", bn),),
                        writes=(hv.R("qT", m),), scale=0.125)
                return f

            def k(m):
                def f(hv, gb):
                    bn = dense8(hv, gb, 4 + m)
                    gblk = 4 * hv.i + 2 * hv.hf
                    col = (gblk % 8) * 128
                    act(kT[:, m, col:col + hv.W], ps[:, bn, 0:hv.W], AF.Copy, reads=(("ps", bn),),
                        writes=(("kT", gblk % 8, m), ("kT", (gblk + 1) % 8, m)))
                    if hv.last:
                        ti = ntb()
                        dve(lambda e: e.tensor_copy(out=tb[:, ti, :], in_=ps[:, bn, 0:hv.W]),
                            reads=(("ps", bn),), writes=(("tb", ti),))
                        out_dma(lambda e: [e.dma_start(out=kpT[m * 128:(m + 1) * 128, hv.cs], in_=tb[:, ti, :])],
                                reads=(("tb", ti),), sem="ot%d" % ti)
                return f

            def v(hv, gb):
                wv = [wget4(gb + 4 * kc, ("w_in", kc, 8)) for kc in range(DC)]
                for t2 in range(2):
                    bn = nbank()
                    c0 = hv.c0 + t2 * 128
                    mm_group(ps[:, bn, :], ("ps", bn),
                             [(ub[:, kc, c0:c0 + 128], hv.R("ub", kc), wv[kc][0], wv[kc][1])
                              for kc in range(DC)])
                    slot = (4 * hv.i + 2 * hv.hf + t2) % 8
                    dve(lambda e, bn=bn, slot=slot: e.tensor_copy(out=Vr[:, slot, :], in_=ps[:, bn, :]),
                        reads=(("ps", bn),), writes=(("V", slot),))
                    if hv.last:
                        for hh in range(2):
                            ti = ntb()
                            act(tb[:, ti, :], ps[:, bn, hh * 256:(hh + 1) * 256], AF.Copy, reads=(("ps", bn),),
                                writes=(("tb", ti),))
                            r0 = (2 * hv.hf + t2) * 128
                            out_dma(lambda e, ti=ti, r0=r0, hh=hh: [e.dma_start(
                                out=vp[r0:r0 + 128, hh * 256:(hh + 1) * 256], in_=tb[:, ti, :])],
                                reads=(("tb", ti),), sem="ot%d" % ti)

            def x(m):
                def f(hv, gb):
                    bn = dense8(hv, gb, 12 + m)
                    act(ysb[:, m, hv.cs], ps[:, bn, 0:hv.W], AF.Copy, reads=(("ps", bn),),
                        writes=(hv.R("ysb", m),))
                return f

            def b(m):
                def f(hv, gb):
                    bn = dense8(hv, gb, 16 + m)
                    act(ysb[:, 4 + m, hv.cs], ps[:, bn, 0:hv.W], AF.Copy, reads=(("ps", bn),),
                        writes=(hv.R("ysb", 4 + m),))
                return f

            def c(m):
                def f(hv, gb):
                    bn = dense8(hv, gb, 20 + m)
                    dve(lambda e: e.tensor_tensor(out=sS[:, m, 2 + hv.c0:2 + hv.c0 + hv.W],
                                                  in0=ysb[:, m, hv.cs], in1=ps[:, bn, 0:hv.W], op=ALU.mult),
                        reads=(hv.R("ysb", m), ("ps", bn)), writes=("sS",))
                return f
            for m in range(4):
                us.append((8, None, q(m)))
            for m in range(4):
                us.append((8, None, k(m)))
            us.append((32, 4.5, v))
            for m in range(4):
                us.append((8, None, x(m)))
            for m in range(4):
                us.append((8, None, b(m)))
            for m in range(4):
                us.append((8, None, c(m)))
            return us

        def u_conv_prompt():
            def cv(c):
                def f(hv, gb):
                    ti = ntb()
                    o = hv.c0
                    W = hv.W
                    dve(lambda e: e.tensor_scalar(out=tb[:, ti, :], in0=sS[:, c, o:o + W],
                                                  scalar1=cw[:, c:c + 1], scalar2=None, op0=ALU.mult),
                        reads=("sS", "cw"), writes=(("tb", ti),))
                    dve(lambda e: e.scalar_tensor_tensor(out=tb[:, ti, :], in0=sS[:, c, o + 1:o + 1 + W],
                                                         scalar=cw[:, 4 + c:5 + c], in1=tb[:, ti, :],
                                                         op0=ALU.mult, op1=ALU.add),
                        reads=("sS", "cw", ("tb", ti)), writes=(("tb", ti),))
                    dve(lambda e: e.scalar_tensor_tensor(out=tb[:, ti, :], in0=sS[:, c, o + 2:o + 2 + W],
                                                         scalar=cw[:, 8 + c:9 + c], in1=tb[:, ti, :],
                                                         op0=ALU.mult, op1=ALU.add),
                        reads=("sS", "cw", ("tb", ti)), writes=(("tb", ti),))
                    dve(lambda e: e.tensor_tensor(out=ycT[:, c, hv.cs], in0=tb[:, ti, :],
                                                  in1=ysb[:, 4 + c, hv.cs], op=ALU.mult),
                        reads=(("tb", ti), hv.R("ysb", 4 + c)), writes=(hv.R("ycT", c),))
                return f

            def fin(hv, gb):
                if hv.hf == 1:
                    if hv.last:
                        out_dma(lambda e: [e.dma_start(out=cpT.rearrange("(c p) j -> p c j", p=128),
                                                       in_=sS[:, :, T:T + 2])], reads=("sS",), sem="o_cp")
                    else:
                        dve(lambda e: e.tensor_copy(out=sS[:, :, 0:2], in_=sS[:, :, T:T + 2]),
                            reads=("sS",), writes=("sS",))
            return [(0, 1.7, cv(c)) for c in range(4)] + [(0, 0.2, fin)]

        def u_attn_prompt():
            us = []

            def mk(qi, ii_pos):
                def f(hv, gb):
                    stt = hv.att
                    steps = stt["steps"]
                    sidx = stt["next"]
                    stt["next"] += 1
                    if sidx >= len(steps):
                        return
                    if sidx == 0:
                        emit_scores(hv, 0)
                    if sidx + 1 < len(steps):
                        emit_scores(hv, sidx + 1)
                    emit_pv(hv, sidx)
                return f
            for k_ in range(10):
                us.append((0, 1.9, mk(0, 0), "att_first" if k_ == 0 else ("att_last" if k_ == 9 else "att")))
            return us

        def att_setup(hv):
            steps = []
            for qq in range(2):
                qb = 2 * hv.hf + qq
                G = 4 * hv.i + qb
                iis = [ii for ii in range(5) if G - 4 + ii >= 0]
                for n_, ii in enumerate(iis):
                    steps.append((qb, G, ii, n_ == 0, n_ == len(iis) - 1))
            hv.att = {"steps": steps, "next": 0}

        def emit_scores(hv, sidx):
            qb, G, ii, first, lastk = hv.att["steps"][sidx]
            gk = G - 4 + ii
            kcol = (gk % 8) * 128
            sb0 = 0 if (sidx % 2 == 0) else 2
            pp = sidx % 2
            for par in range(2):
                bn = sb0 + par
                tr.op("pe", lambda e, bn=bn, par=par: e.matmul(
                    ps[:, bn, :], lhsT=ident[:], rhs=biasT[:, ii, par * 512:(par + 1) * 512],
                    start=True, stop=False, skip_group_check=True),
                    reads=("ident", "biasT"), writes=(("ps", bn),), inc=False)
            for hc in range(4):
                for par in range(2):
                    bn = sb0 + par
                    tr.op("pe", lambda e, bn=bn, par=par, hc=hc: e.matmul(
                        ps[:, bn, hc * 128:(hc + 1) * 128],
                        lhsT=kT[par * 64:(par + 1) * 64, hc, kcol:kcol + 128],
                        rhs=qT[par * 64:(par + 1) * 64, hc, qb * 128:(qb + 1) * 128],
                        start=False, stop=True, skip_group_check=True),
                        reads=(("kT", gk % 8, hc), hv.R("qT", hc)), writes=(("ps", bn),),
                        inc=(hc == 3 and par == 1))
            act(PT[:, pp, :].rearrange("p (a b) -> p a b", a=2), ps[:, sb0:sb0 + 2, :], AF.Exp,
                reads=(("ps", sb0), ("ps", sb0 + 1)), writes=(("PT", pp),))

        def emit_pv(hv, sidx):
            qb, G, ii, first, lastk = hv.att["steps"][sidx]
            gk = G - 4 + ii
            vslot = gk % 8
            pp = sidx % 2
            bs_, bo_ = 5, 6
            for par in range(2):
                tr.op("pe", lambda e, par=par: e.matmul(
                    ps[par * 64:(par + 1) * 64, bs_, :], lhsT=ones1[:, 0:64],
                    rhs=PT[:, pp, par * 512:(par + 1) * 512], start=first, stop=lastk),
                    reads=("ones1", ("PT", pp)), writes=(("ps", bs_),), inc=False)
            for h_ in range(8):
                par, hc = h_ % 2, h_ // 2
                tr.op("pe", lambda e, par=par, hc=hc, h_=h_: e.matmul(
                    ps[par * 64:(par + 1) * 64, bo_, hc * 128:(hc + 1) * 128],
                    lhsT=Vr[:, vslot, h_ * 64:(h_ + 1) * 64],
                    rhs=PT[:, pp, par * 512 + hc * 128:par * 512 + (hc + 1) * 128],
                    start=(first and hc == 0), stop=lastk, skip_group_check=True),
                    reads=(("V", vslot), ("PT", pp)), writes=(("ps", bo_),), inc=(h_ == 7))
            if lastk:
                act(Rr[:], ps[:, bs_, :], AF.Ln, reads=(("ps", bs_),), writes=("Rr",))
                act(Rr[:], Rr[:], AF.Exp, reads=("Rr",), writes=("Rr",), scale=-1.0)
                dve(lambda e: e.tensor_tensor(
                    out=oT[:, :, qb * 128:(qb + 1) * 128],
                    in0=ps[:, bo_, :].rearrange("p (c q) -> p c q", q=128),
                    in1=Rr[:].rearrange("p (c q) -> p c q", q=128), op=ALU.mult),
                    reads=(("ps", bo_), "Rr"), writes=tuple(hv.R("oT", c) for c in range(4)))

        def u_gating():
            us = []

            def gate(m):
                def f(hv, gb):
                    W = hv.W
                    b1, b2 = nbank(), nbank()
                    terms = []
                    for kc in range(4):
                        w, wres = wget(gb + kc, ("w_att_out", kc, m))
                        terms.append((w, wres, oT[:, kc, hv.cs], hv.R("oT", kc)))
                    mm_group(ps[:, b1, 0:W], ("ps", b1), terms)
                    terms = []
                    for kc in range(4):
                        w, wres = wget(gb + 4 + kc, ("w_conv_out", kc, m))
                        terms.append((w, wres, ycT[:, kc, hv.cs], hv.R("ycT", kc)))
                    mm_group(ps[:, b1, 256:256 + W], ("ps", b1), terms)
                    terms = []
                    for kc in range(DC):
                        w, wres = wget(gb + 8 + kc, ("w_in", kc, 24 + m))
                        terms.append((w, wres, ub[:, kc, hv.cs], hv.R("ub", kc)))
                    mm_group(ps[:, b2, 0:W], ("ps", b2), terms)
                    terms = []
                    for kc in range(DC):
                        w, wres = wget(gb + 16 + kc, ("w_in", kc, 32 + m))
                        terms.append((w, wres, ub[:, kc, hv.cs], hv.R("ub", kc)))
                    mm_group(ps[:, b2, 256:256 + W], ("ps", b2), terms)
                    t1, t2 = ntb(), ntb()
                    act(tb[:, t1, 0:W], ps[:, b2, 0:W], AF.Tanh, reads=(("ps", b2),), writes=(("tb", t1),),
                        scale=0.5)
                    act(tb[:, t2, 0:W], ps[:, b2, 256:256 + W], AF.Tanh, reads=(("ps", b2),),
                        writes=(("tb", t2),), scale=0.5)
                    dve(lambda e: e.scalar_tensor_tensor(out=tb[:, t1, 0:W], in0=tb[:, t1, 0:W], scalar=1.0,
                                                         in1=ps[:, b1, 0:W], op0=ALU.add, op1=ALU.mult),
                        reads=(("tb", t1), ("ps", b1)), writes=(("tb", t1),))
                    dve(lambda e: e.scalar_tensor_tensor(out=tb[:, t2, 0:W], in0=tb[:, t2, 0:W], scalar=1.0,
                                                         in1=ps[:, b1, 256:256 + W], op0=ALU.add, op1=ALU.mult),
                        reads=(("tb", t2), ("ps", b1)), writes=(("tb", t2),))
                    dve(lambda e: e.tensor_tensor(out=mTb[:, m, hv.cs], in0=tb[:, t1, 0:W],
                                                  in1=tb[:, t2, 0:W], op=ALU.add),
                        reads=(("tb", t1), ("tb", t2)), writes=(hv.R("mT", m),))
                return f

            def outp(m):
                def f(hv, gb):
                    bn = nbank()
                    terms = []
                    for kc in range(DC):
                        w, wres = wget(gb + kc, ("w_out", kc, m))
                        terms.append((w, wres, mTb[:, kc, hv.cs], hv.R("mT", kc)))
                    mm_group(ps[:, bn, 0:hv.W], ("ps", bn), terms)
                    evac_sq(hv, bn, m, 3, False, doubled=True)
                return f
            for m in range(DC):
                us.append((24, None, gate(m)))
            for m in range(DC):
                us.append((8, None, outp(m)))
            return us

        def u_ple():
            us = []

            def hb(c):
                def f(hv, gb):
                    act(ub[:, c, hv.cs], hv.h[:, c, hv.cs], AF.Copy, reads=(hv.H(c),), writes=(hv.R("ub", c),))
                return f

            def pl(m):
                def f(hv, gb):
                    W = hv.W
                    bn = nbank()
                    terms = []
                    for kc in range(DC):
                        w, wres = wget(gb + kc, ("w_ple_gate", kc, m))
                        terms.append((w, wres, ub[:, kc, hv.cs], hv.R("ub", kc)))
                    mm_group(ps[:, bn, 0:W], ("ps", bn), terms)
                    terms = []
                    for kc in range(2):
                        w, wres = wget(gb + 8 + kc, ("w_ple_proj", kc, m))
                        terms.append((w, wres, hv.pb[:, kc, hv.cs], hv.pres))
                    mm_group(ps[:, bn, 256:256 + W], ("ps", bn), terms)
                    ti = ntb()
                    act(tb[:, ti, 0:W], ps[:, bn, 0:W], AF.Tanh, reads=(("ps", bn),), writes=(("tb", ti),),
                        scale=0.5)
                    dve(lambda e: e.scalar_tensor_tensor(out=ysb[:, m, hv.cs], in0=tb[:, ti, 0:W], scalar=1.0,
                                                         in1=ps[:, bn, 256:256 + W], op0=ALU.add, op1=ALU.mult),
                        reads=(("tb", ti), ("ps", bn)), writes=(hv.R("ysb", m),))
                    act(sq[:, m, hv.cs], ysb[:, m, hv.cs], AF.Square, reads=(hv.R("ysb", m),),
                        writes=(hv.R("sq", m),), scale=0.5)
                return f
            def un(c):
                def f(hv, gb):
                    if hv.next is not None:
                        u_op(hv.next, hv, 0, c, True)
                return f
            for m in range(DC):
                us.append((10, None, pl(m)))
            return us

        def u_output():
            def f(hv, gb):
                hres = hv.hres
                if hv.sample:
                    out_dma(lambda e: [e.dma_start(out=ysT.rearrange("(c p) t -> p c t", p=128),
                                                   in_=hv.h[:, :, 0:TS])],
                            reads=tuple(hv.H(c) for c in range(DC)), sem="oh%d" % (hv.i % 2))
                else:
                    t0 = hv.i * T + hv.c0
                    dst = yT.rearrange("(c p) t -> p c t", p=128)[:, :, t0:t0 + hv.W]
                    out_dma(lambda e: [e.dma_start(out=dst, in_=hv.h[:, :, hv.cs])],
                            reads=tuple(hv.H(c) for c in range(DC)), sem="oh%d" % (hv.i % 2))
            return [(0, 0.1, f, "output")]

        def u_inproj_sample():
            us = []

            def pre(hv, gb):
                load_cache(0)
                load_cache(1)
                tr.dma("sp", lambda e: [e.dma_start(out=sSs[:, :, :, 0:2], in_=ccTd)], "cst2",
                       writes=("sSs",))

            def q(m):
                def f(hv, gb):
                    bn = dense8(hv, gb, m)
                    act(qT[:, m, hv.cs], ps[:, bn, 0:hv.W], AF.Copy, reads=(("ps", bn),),
                        writes=(hv.R("qT", m),), scale=0.125)
                return f

            def k(m):
                def f(hv, gb):
                    bn = dense8(hv, gb, 4 + m)
                    act(mTb[:, m, 0:TS], ps[:, bn, 0:TS], AF.Copy, reads=(("ps", bn),), writes=(hv.R("mT", m),))
                    ti = ntb()
                    dve(lambda e: e.tensor_copy(out=tb[:, ti, 0:TS], in_=ps[:, bn, 0:TS]),
                        reads=(("ps", bn),), writes=(("tb", ti),))
                    out_dma(lambda e: [e.dma_start(out=ksT[m * 128:(m + 1) * 128, :], in_=tb[:, ti, 0:TS])],
                            reads=(("tb", ti),), sem="ot%d" % ti)
                return f

            def v(hv, gb):
                wv = [wget4(gb + 4 * kc, ("w_in", kc, 8)) for kc in range(DC)]
                for b in range(4):
                    bn = nbank()
                    mm_group(ps[0:16, bn, :], ("ps", bn),
                             [(ub[:, kc, b * 16:(b + 1) * 16], hv.R("ub", kc), wv[kc][0], wv[kc][1])
                              for kc in range(DC)])
                    dve(lambda e, b=b, bn=bn: e.tensor_copy(out=vnew[:, b, :], in_=ps[0:16, bn, :]),
                        reads=(("ps", bn),), writes=("vnew",))
                    for hh in range(2):
                        ti = ntb()
                        act(tb[0:16, ti, :], ps[0:16, bn, hh * 256:(hh + 1) * 256], AF.Copy,
                            reads=(("ps", bn),), writes=(("tb", ti),))
                        out_dma(lambda e, b=b, ti=ti, hh=hh: [e.dma_start(
                            out=vs[b, :, hh * 256:(hh + 1) * 256], in_=tb[0:16, ti, :])],
                            reads=(("tb", ti),), sem="ot%d" % ti)

            def x(m):
                def f(hv, gb):
                    bn = dense8(hv, gb, 12 + m)
                    act(ysb[:, m, 0:TS], ps[:, bn, 0:TS], AF.Copy, reads=(("ps", bn),), writes=(hv.R("ysb", m),))
                return f

            def b_(m):
                def f(hv, gb):
                    bn = dense8(hv, gb, 16 + m)
                    act(ysb[:, 4 + m, 0:TS], ps[:, bn, 0:TS], AF.Copy, reads=(("ps", bn),),
                        writes=(hv.R("ysb", 4 + m),))
                return f

            def c(m):
                def f(hv, gb):
                    bn = dense8(hv, gb, 20 + m)
                    dve(lambda e: e.tensor_tensor(
                        out=sSs[:, m, :, 2:18],
                        in0=ysb[:, m, 0:TS].rearrange("p (b t) -> p b t", t=16),
                        in1=ps[:, bn, 0:TS].rearrange("p (b t) -> p b t", t=16), op=ALU.mult),
                        reads=(hv.R("ysb", m), ("ps", bn)), writes=("sSs",))
                return f
            us.append((0, 0.1, pre))
            for m in range(4):
                us.append((8, None, q(m)))
            for m in range(4):
                us.append((8, None, k(m)))
            us.append((32, 4.5, v))
            for m in range(4):
                us.append((8, None, x(m)))
            for m in range(4):
                us.append((8, None, b_(m)))
            for m in range(4):
                us.append((8, None, c(m)))
            return us

        def load_cache(b):
            hf = b % 2
            tr.dma("pool", lambda e, b=b, hf=hf: [
                e.dma_start(out=kT[:, :, hf * 512:(hf + 1) * 512], in_=ckTd[b]),
                e.dma_start(out=Vr[:, hf * 4:(hf + 1) * 4, :],
                            in_=cvd[b].rearrange("(k p) f -> p k f", p=128))],
                "ch%d" % hf, writes=tuple(("kT", hf * 4 + k4, m) for m in range(4) for k4 in range(4)) +
                tuple(("V", hf * 4 + k) for k in range(4)), n=2)

        def u_attn_sample():
            def seq(b):
                def f(hv, gb):
                    kTs = mTb
                    hf = b % 2
                    bs_, bo_ = nbank(), nbank()
                    for blk in range(5):
                        KP = 128 if blk < 4 else 16
                        for par in range(2):
                            bn = nbank()
                            if blk < 4:
                                tr.op("pe", lambda e, bn=bn, blk=blk, par=par: e.matmul(
                                    ps[:, bn, 0:64], lhsT=ident[:], rhs=biasS[:, blk, par * 64:(par + 1) * 64],
                                    start=True, stop=False, skip_group_check=True),
                                    reads=("ident", "biasS"), writes=(("ps", bn),), inc=False)
                            else:
                                tr.op("pe", lambda e, bn=bn, par=par: e.matmul(
                                    ps[0:16, bn, 0:64], lhsT=ident[0:16, 0:16],
                                    rhs=biasN[:, par * 64:(par + 1) * 64], start=True, stop=False,
                                    skip_group_check=True),
                                    reads=("ident", "biasN"), writes=(("ps", bn),), inc=False)
                            for hc in range(4):
                                if blk < 4:
                                    l = kT[par * 64:(par + 1) * 64, hc,
                                           hf * 512 + blk * 128:hf * 512 + (blk + 1) * 128]
                                    lres = ("kT", hf * 4 + blk, hc)
                                else:
                                    l = kTs[par * 64:(par + 1) * 64, hc, b * 16:(b + 1) * 16]
                                    lres = hv.R("mT", hc)
                                tr.op("pe", lambda e, bn=bn, l=l, par=par, hc=hc, KP=KP: e.matmul(
                                    ps[0:KP, bn, hc * 16:(hc + 1) * 16], lhsT=l,
                                    rhs=qT[par * 64:(par + 1) * 64, hc, b * 16:(b + 1) * 16],
                                    start=False, stop=True, skip_group_check=True),
                                    reads=(lres, hv.R("qT", hc)), writes=(("ps", bn),), inc=(hc == 3))
                            act(PTs[0:KP, blk, par * 64:(par + 1) * 64], ps[0:KP, bn, 0:64], AF.Exp,
                                reads=(("ps", bn),), writes=(("PTs", blk),))
                    for blk in range(5):
                        KP = 128 if blk < 4 else 16
                        for par in range(2):
                            tr.op("pe", lambda e, par=par, blk=blk, KP=KP: e.matmul(
                                ps[par * 64:(par + 1) * 64, bs_, 0:64], lhsT=ones1[0:KP, 0:64],
                                rhs=PTs[0:KP, blk, par * 64:(par + 1) * 64], start=(blk == 0), stop=(blk == 4)),
                                reads=("ones1", ("PTs", blk)), writes=(("ps", bs_),), inc=False)
                        for h_ in range(8):
                            par, hc = h_ % 2, h_ // 2
                            if blk < 4:
                                l = Vr[:, hf * 4 + blk, h_ * 64:(h_ + 1) * 64]
                                lres = ("V", hf * 4 + blk)
                            else:
                                l = vnew[0:16, b, h_ * 64:(h_ + 1) * 64]
                                lres = "vnew"
                            tr.op("pe", lambda e, l=l, par=par, hc=hc, blk=blk, KP=KP: e.matmul(
                                ps[par * 64:(par + 1) * 64, bo_, hc * 16:(hc + 1) * 16], lhsT=l,
                                rhs=PTs[0:KP, blk, par * 64 + hc * 16:par * 64 + (hc + 1) * 16],
                                start=(blk == 0 and hc == 0), stop=(blk == 4), skip_group_check=True),
                                reads=(lres, ("PTs", blk)), writes=(("ps", bo_),), inc=(h_ == 7))
                    act(Rr[:, 0:64], ps[:, bs_, 0:64], AF.Ln, reads=(("ps", bs_),), writes=("Rr",))
                    act(Rr[:, 0:64], Rr[:, 0:64], AF.Exp, reads=("Rr",), writes=("Rr",), scale=-1.0)
                    dve(lambda e: e.tensor_tensor(
                        out=oT[:, :, b * 16:(b + 1) * 16],
                        in0=ps[:, bo_, 0:64].rearrange("p (c q) -> p c q", q=16),
                        in1=Rr[:, 0:64].rearrange("p (c q) -> p c q", q=16), op=ALU.mult),
                        reads=(("ps", bo_), "Rr"), writes=tuple(hv.R("oT", c) for c in range(4)))
                    if b + 2 < 4:
                        load_cache(b + 2)
                return f

            def cv(c):
                def f(hv, gb):
                    ti = ntb()
                    tv = tb[:, ti, 0:TS].rearrange("p (b t) -> p b t", t=16)
                    dve(lambda e: e.tensor_scalar(out=tv, in0=sSs[:, c, :, 0:16], scalar1=cw[:, c:c + 1],
                                                  scalar2=None, op0=ALU.mult),
                        reads=("sSs", "cw"), writes=(("tb", ti),))
                    dve(lambda e: e.scalar_tensor_tensor(out=tv, in0=sSs[:, c, :, 1:17],
                                                         scalar=cw[:, 4 + c:5 + c], in1=tv,
                                                         op0=ALU.mult, op1=ALU.add),
                        reads=("sSs", "cw", ("tb", ti)), writes=(("tb", ti),))
                    dve(lambda e: e.scalar_tensor_tensor(out=tv, in0=sSs[:, c, :, 2:18],
                                                         scalar=cw[:, 8 + c:9 + c], in1=tv,
                                                         op0=ALU.mult, op1=ALU.add),
                        reads=("sSs", "cw", ("tb", ti)), writes=(("tb", ti),))
                    dve(lambda e: e.tensor_tensor(out=ycT[:, c, 0:TS], in0=tb[:, ti, 0:TS],
                                                  in1=ysb[:, 4 + c, 0:TS], op=ALU.mult),
                        reads=(("tb", ti), hv.R("ysb", 4 + c)), writes=(hv.R("ycT", c),))
                return f

            def fin(hv, gb):
                out_dma(lambda e: [e.dma_start(out=csT[c * 128:(c + 1) * 128], in_=sSs[:, c, :, 16:18])
                                   for c in range(4)], reads=("sSs",), sem="o_cs", n=4)
            return [(0, 3.0, seq(b)) for b in range(4)] + [(0, 0.5, cv(c)) for c in range(4)] + [(0, 0.1, fin)]

        def pass_units(sample):
            us = []
            us += u_ffn("w1", 1, from_m=True)
            us += u_post(1, True) + u_pre_fin(2)
            if sample:
                us += u_inproj_sample() + u_attn_sample()
            else:
                us += u_inproj_prompt() + u_conv_prompt() + u_attn_prompt()
            us += u_gating()
            us += u_post(3, True) + u_pre_fin(4)
            us += u_ffn("w2", 5, with_next=True)
            us += u_post(5, False, then_hb=True)
            us += u_ple()
            us += u_post(6, False, g_in_chain=(gh, "gh", 6))
            us += u_output()
            return us

        PROLOGUE = u_pre_sq(0) + u_pre_fin(0, to_m=True)
        PU = pass_units(False)
        PUS = pass_units(True)
        assert sum(u[0] for u in PU) == NBLK and sum(u[0] for u in PUS) == NBLK, \
            (sum(u[0] for u in PU), sum(u[0] for u in PUS), NBLK)

        def ucost(u):
            return u[1] if u[1] is not None else u[0] * 0.135

        def load_tile(i):
            hb = hbuf[i % 2]
            hres = "h%d" % (i % 2)
            src = xT.rearrange("(c p) t -> p c t", p=128)[:, :, i * T:(i + 1) * T]
            tr.dma("pool", lambda e: [e.dma_start(out=hb[:], in_=src)], "hs%d" % (i % 2),
                   writes=tuple((hres, hf, c) for c in range(DC) for hf in range(2)))
            psrc = pT.rearrange("(c p) t -> p c t", p=128)[:, :, i * T:(i + 1) * T]
            pb = pTb[i % 2]
            tr.dma("pool", lambda e: [e.dma_start(out=pb[:], in_=psrc)], "ps%d" % (i % 2),
                   writes=("p%d" % (i % 2),))

        def load_sample(i):
            hb = hbuf[i % 2]
            hres = "h%d" % (i % 2)
            src = xsT.rearrange("(c p) t -> p c t", p=128)
            tr.dma("pool", lambda e: [e.dma_start(out=hb[:, :, 0:TS], in_=src)], "hs%d" % (i % 2),
                   writes=tuple((hres, hf, c) for c in range(DC) for hf in range(2)))
            psrc = psT.rearrange("(c p) t -> p c t", p=128)
            pb = pTb[i % 2]
            tr.dma("pool", lambda e: [e.dma_start(out=pb[:, :, 0:TS], in_=psrc)], "ps%d" % (i % 2),
                   writes=("p%d" % (i % 2),))

        NTAIL = 10

        def make_stream(hf):
            sl = []
            hvs = [Half(i, hf) for i in range(NT)]
            for i in range(NT):
                hvs[i].next = hvs[i + 1] if i + 1 < NT else None
            pending = []
            for i in range(NT):
                hv = hvs[i]
                gb = hv.gb0
                if i == 0:
                    for u in PROLOGUE:
                        sl.append((hv, u, gb))
                main, tailu = PU[:-NTAIL], PU[-NTAIL:]
                for k_, u in enumerate(main):
                    sl.append((hv, u, gb))
                    gb += u[0]
                    if pending and u[0] > 0:
                        sl.append(pending.pop(0))
                assert not pending
                assert all(u[0] == 0 for u in tailu)
                pending = [(hv, u, gb) for u in tailu]
            sl += pending
            return sl
        SA, SB = make_stream(0), make_stream(1)
        if with_sample:
            hvS = Half(NT, 0, sample=True)
            gb = hvS.gb0
            hvS.next = None
            SS = [(hvS, u, gb) for u in PROLOGUE]
            for u in PUS:
                SS.append((hvS, u, gb))
                gb += u[0]
        else:
            SS = []

        def load_index(i):
            if i < NT:
                load_tile(i)
            elif i == NT and with_sample:
                load_sample(i)

        load_index(0)
        load_index(1)

        def emit(entry):
            hv, u, gb = entry
            if not hv.sample and not hasattr(hv, "att"):
                att_setup(hv)
            u[2](hv, gb)
            if len(u) > 3 and u[3] == "output" and not hv.sample and hv.hf == 1:
                load_index(hv.i + 2)

        ia = ib = 0
        ta = tb_ = 0.0
        LEAD = 12.0
        in_att = [None]

        def kind(ent):
            return ent[1][3] if len(ent[1]) > 3 else ""

        def note(which, ent):
            k = kind(ent)
            if k == "att_first":
                in_att[0] = which
            elif k == "att_last":
                in_att[0] = None
        while ia < len(SA) or ib < len(SB):
            gB = (SB[ib][2] // SLAB) if ib < len(SB) else (total_slabs if not SS else NT * NSLAB)
            ensure_slabs(gB + NSLOT - 1)
            pickA = False
            if ia < len(SA):
                ua = SA[ia]
                lastslab = (ua[2] + max(ua[1][0], 1) - 1) // SLAB
                fits = lastslab <= gB + NSLOT - 1
                if ib >= len(SB):
                    assert fits
                    pickA = True
                elif fits and (ta - tb_) < LEAD and lastslab <= gB + NSLOT - 3:
                    pickA = True
            if in_att[0] == "A":
                pickA = True
            elif in_att[0] == "B":
                pickA = False
            if pickA:
                note("A", SA[ia])
                emit(SA[ia])
                ta += ucost(SA[ia][1])
                ia += 1
            else:
                note("B", SB[ib])
                emit(SB[ib])
                tb_ += ucost(SB[ib][1])
                ib += 1
        for k_, ent in enumerate(SS):
            gS = ent[2] // SLAB
            ensure_slabs(gS + NSLOT - 1)
            emit(ent)
        tr.final_wait("sp", OUT_SEMS)
        for e_ in tr.CE:
            assert not tr.dangling[e_], e_

        def replay(name, eng):
            for item in tr.prog[name]:
                if item[0] == "wait":
                    eng.wait_ge(sems[item[1]], item[2])
                elif item[0] == "op":
                    ins = item[1](eng)
                    if item[2] is not None:
                        ins.then_inc(sems[item[2]], 1)
                else:
                    for ins in item[1](eng):
                        ins.then_inc(sems[item[2]], 16)

        with nc.Block() as block:
            @block.tensor
            def _(e):
                replay("pe", e)

            @block.scalar
            def _(e):
                replay("act", e)

            @block.vector
            def _(e):
                replay("dve", e)

            @block.gpsimd
            def _(e):
                replay("pool", e)

            @block.sync
            def _(e):
                replay("sp", e)
    return nc


def _weights_stream(W):
    wall = np.empty((NSLAB, 128, SLAB * 128), np.float32)
    for b, (name, kc, mc) in enumerate(BLOCKS):
        g, o = divmod(b, SLAB)
        wall[g, :, o * 128:(o + 1) * 128] = W[name][kc * 128:(kc + 1) * 128, mc * 128:(mc + 1) * 128]
    return wall


def _bias_tiles(table):
    kk = np.arange(128)[:, None]
    qq = np.arange(128)[None, :]
    bp = np.empty((128, 5, 1024), np.float32)
    for ii in range(5):
        d = np.clip(128 * (4 - ii) + qq - kk, -128, 128) + 128
        for h in range(8):
            par, hc = h % 2, h // 2
            bp[:, ii, par * 512 + hc * 128:par * 512 + (hc + 1) * 128] = table[h][d]
    tt = np.arange(16)[None, :]
    bs = np.empty((128, 4, 128), np.float32)
    for blk in range(4):
        d = np.clip(512 + tt - (128 * blk + kk), -128, 128) + 128
        for h in range(8):
            par, hc = h % 2, h // 2
            bs[:, blk, par * 64 + hc * 16:par * 64 + (hc + 1) * 16] = table[h][d]
    bn = np.empty((16, 128), np.float32)
    d = np.clip(tt - np.arange(16)[:, None], -128, 128) + 128
    for h in range(8):
        par, hc = h % 2, h // 2
        bn[:, par * 64 + hc * 16:par * 64 + (hc + 1) * 16] = table[h][d]
    return bp, bs, bn


_NC_CACHE = {}


def _prep_inputs(inp, NT):
    f = lambda a: np.asarray(a, dtype=np.float32)
    W = {k: f(inp[k])[0] for k in ("w1_gate", "w1_up", "w1_down", "w_in", "w_att_out", "w_conv_out",
                                   "w_out", "w2_gate", "w2_up", "w2_down", "w_ple_gate", "w_ple_proj")}
    wall = _weights_stream(W)
    g = f(inp["norm_g"])[0]
    gv = np.ascontiguousarray(g.reshape(7, 8, 128).transpose(2, 0, 1).reshape(128, 56))
    cwv = f(inp["conv_w"])[0]
    cw = np.ascontiguousarray(cwv.reshape(3, 4, 128).transpose(2, 0, 1).reshape(128, 12))
    bp, bs, bn = _bias_tiles(f(inp["rel_bias"])[0])
    ident = np.eye(128, dtype=np.float32)
    xp, xs = f(inp["x_prompt"]), f(inp["x_sample"])
    pp, psm = f(inp["p_prompt"])[0], f(inp["p_sample"])[0]
    ck, cv, cc = f(inp["cache_k"])[0], f(inp["cache_v"])[0], f(inp["cache_conv"])[0]
    SP = NT * T
    maps = []
    for c in range(NCORES):
        sl = slice(4 * c, 4 * c + 4)
        ckc = ck[sl]
        ckT = np.ascontiguousarray(ckc.reshape(4, 512, 4, 2, 64).transpose(0, 3, 4, 2, 1)).reshape(4, 128, 4, 512)
        ccT = np.ascontiguousarray(cc[sl].reshape(4, 2, 4, 128).transpose(3, 2, 0, 1))
        maps.append({
            "xT": np.ascontiguousarray(xp[c, :SP].T),
            "pT": np.ascontiguousarray(pp[c, :SP].T),
            "xsT": np.ascontiguousarray(xs[sl].reshape(TS, D).T),
            "psT": np.ascontiguousarray(psm[sl].reshape(TS, 256).T),
            "wall": wall, "gv": gv, "cw": cw, "biasp": bp, "biass": bs, "biasn": bn, "ident": ident,
            "ckT": ckT, "cv": np.ascontiguousarray(cv[sl].reshape(4, 512, 512)), "ccT": ccT,
        })
    return maps


def _run(inp, NT=16, with_sample=True, trace=False, stop=None):
    key = (NT, with_sample, stop)
    if key not in _NC_CACHE:
        _NC_CACHE[key] = build(NT, with_sample, stop)
    nc = _NC_CACHE[key]
    maps = _prep_inputs(inp, NT)
    res = run_bass_kernel_spmd(nc, maps, core_ids=list(range(NCORES)), trace=trace)
    R = res.results
    SP = NT * T
    y_p = np.stack([R[c]["yT"].T for c in range(NCORES)])
    y_s = np.concatenate([R[c]["ysT"].T.reshape(4, 16, D) for c in range(NCORES)])
    k_p = np.stack([R[c]["kpT"].T.reshape(512, 8, 64) for c in range(NCORES)])[None]
    v_p = np.stack([R[c]["vp"].reshape(512, 8, 64) for c in range(NCORES)])[None]
    c_p = np.stack([R[c]["cpT"].T for c in range(NCORES)])[None]
    k_s = np.concatenate([R[c]["ksT"].T.reshape(4, 16, 8, 64) for c in range(NCORES)])[None]
    v_s = np.concatenate([R[c]["vs"].reshape(4, 16, 8, 64) for c in range(NCORES)])[None]
    c_s = np.concatenate([R[c]["csT"].transpose(1, 2, 0) for c in range(NCORES)])[None]
    outs = (y_p, y_s, k_p, v_p, c_p, k_s, v_s, c_s)
    return tuple(np.ascontiguousarray(o, dtype=np.float32) for o in outs), res


def kernel(**inputs):
    outs, _ = _run(inputs, NT=SEQ // T, with_sample=True)
    return outs
```

```python
import numpy as np
import concourse.bass as bass
import concourse.mybir as mybir
from concourse.bass_utils import run_bass_kernel_spmd
from contextlib import ExitStack

F32 = mybir.dt.float32
BF16 = mybir.dt.bfloat16
AF = mybir.ActivationFunctionType
ALU = mybir.AluOpType

NCORES = 8
D = 1024
DC = 8
DFF = 2816
FC = 22
T = 512
TS = 64
SEQ = 8192
NCV = 8
NSLOT = 10
SLAB = 16
EPS = 1e-6
NEG = -30000.0


def _block_list():
    bl = []

    def ffn(tag):
        for j in range(FC):
            for kc in range(DC):
                bl.append((tag + "_gate", kc, j))
            for kc in range(DC):
                bl.append((tag + "_up", kc, j))
        for m in range(DC):
            for j in range(FC):
                bl.append((tag + "_down", j, m))

    ffn("w1")
    for m in range(8):
        for kc in range(DC):
            bl.append(("w_in", kc, m))
    for kc in range(DC):
        for m in range(8, 12):
            bl.append(("w_in", kc, m))
    for m in range(12, 24):
        for kc in range(DC):
            bl.append(("w_in", kc, m))
    for m in range(DC):
        for kc in range(4):
            bl.append(("w_att_out", kc, m))
        for kc in range(4):
            bl.append(("w_conv_out", kc, m))
        for kc in range(DC):
            bl.append(("w_in", kc, 24 + m))
        for kc in range(DC):
            bl.append(("w_in", kc, 32 + m))
    for m in range(DC):
        for kc in range(DC):
            bl.append(("w_out", kc, m))
    ffn("w2")
    for m in range(DC):
        for kc in range(DC):
            bl.append(("w_ple_gate", kc, m))
        for kc in range(2):
            bl.append(("w_ple_proj", kc, m))
    return bl


BLOCKS = _block_list()
NBLK = len(BLOCKS)
assert NBLK % SLAB == 0
NSLAB = NBLK // SLAB


class Tracker:
    CE = ("pe", "act", "dve", "pool")

    def __init__(self):
        self.prog = {e: [] for e in ("pe", "act", "dve", "pool", "sp")}
        self.cnt = {e: 0 for e in self.CE}
        self.dangling = {e: False for e in self.CE}
        self.known = {e: {} for e in self.prog}
        self.last_w = {}
        self.readers = {}
        self.dcnt = {}

    def _deps(self, eng, reads, writes):
        need = {}

        def add(tok):
            if tok is None:
                return
            s, v = tok
            if need.get(s, 0) < v:
                need[s] = v

        for r in reads:
            add(self.last_w.get(r))
            if isinstance(r, tuple) and r[0] == "ps":
                for s, v in self.readers.get(r, {}).items():
                    if s != eng:
                        add((s, v))
        for w in writes:
            add(self.last_w.get(w))
            for s, v in self.readers.get(w, {}).items():
                add((s, v))
        kn = self.known[eng]
        for s, v in need.items():
            if eng == "pe" and s == "pe":
                continue
            if kn.get(s, 0) >= v:
                continue
            kn[s] = v
            self.prog[eng].append(("wait", s, v))

    def _commit(self, tok, reads, writes):
        for r in reads:
            d = self.readers.setdefault(r, {})
            if d.get(tok[0], 0) < tok[1]:
                d[tok[0]] = tok[1]
        for w in writes:
            self.last_w[w] = tok
            self.readers[w] = {}

    def op(self, eng, fn, reads=(), writes=(), inc=True):
        self._deps(eng, reads, writes)
        if inc:
            self.cnt[eng] += 1
            tok = (eng, self.cnt[eng])
            self.dangling[eng] = False
            self.prog[eng].append(("op", fn, eng, 1))
        else:
            tok = (eng, self.cnt[eng] + 1)
            self.dangling[eng] = True
            self.prog[eng].append(("op", fn, None, 0))
        self._commit(tok, reads, writes)

    def dma(self, queue, fn, sem, reads=(), writes=(), n=1):
        self._deps(queue, reads, writes)
        self.dcnt[sem] = self.dcnt.get(sem, 0) + 16 * n
        tok = (sem, self.dcnt[sem])
        self.prog[queue].append(("dma", fn, sem, n))
        self._commit(tok, reads, writes)

    def final_wait(self, queue, sems):
        for s in sems:
            if s in self.dcnt:
                self.prog[queue].append(("wait", s, self.dcnt[s]))


class _Stop(Exception):
    pass


def build(NT=16, with_sample=True, stop=None):
    nc = bass.Bass("TRN2", target_bir_lowering=False)
    SP = NT * T
    HW = T // 2
    xT = nc.dram_tensor("xT", [D, SP], F32, kind="ExternalInput").ap()
    pT = nc.dram_tensor("pT", [256, SP], F32, kind="ExternalInput").ap()
    xsT = nc.dram_tensor("xsT", [D, TS], F32, kind="ExternalInput").ap()
    psT = nc.dram_tensor("psT", [256, TS], F32, kind="ExternalInput").ap()
    wall = nc.dram_tensor("wall", [NSLAB, 128, SLAB * 128], F32, kind="ExternalInput").ap()
    wbf = nc.dram_tensor("wbf", [NSLAB, 128, SLAB * 128], BF16, kind="Internal").ap()
    gvd = nc.dram_tensor("gv", [128, 56], F32, kind="ExternalInput").ap()
    cwd = nc.dram_tensor("cw", [128, 12], F32, kind="ExternalInput").ap()
    biasd = nc.dram_tensor("biasp", [128, 5, 1024], F32, kind="ExternalInput").ap()
    biassd = nc.dram_tensor("biass", [128, 4, 128], F32, kind="ExternalInput").ap()
    biasnd = nc.dram_tensor("biasn", [16, 128], F32, kind="ExternalInput").ap()
    identd = nc.dram_tensor("ident", [128, 128], F32, kind="ExternalInput").ap()
    ckTd = nc.dram_tensor("ckT", [4, 128, 4, 512], F32, kind="ExternalInput").ap()
    cvd = nc.dram_tensor("cv", [4, 512, 512], F32, kind="ExternalInput").ap()
    ccTd = nc.dram_tensor("ccT", [128, 4, 4, 2], F32, kind="ExternalInput").ap()

    yT = nc.dram_tensor("yT", [D, SP], F32, kind="ExternalOutput").ap()
    ysT = nc.dram_tensor("ysT", [D, TS], F32, kind="ExternalOutput").ap()
    kpT = nc.dram_tensor("kpT", [512, 512], F32, kind="ExternalOutput").ap()
    vp = nc.dram_tensor("vp", [512, 512], F32, kind="ExternalOutput").ap()
    cpT = nc.dram_tensor("cpT", [512, 2], F32, kind="ExternalOutput").ap()
    ksT = nc.dram_tensor("ksT", [512, TS], F32, kind="ExternalOutput").ap()
    vs = nc.dram_tensor("vs", [4, 16, 512], F32, kind="ExternalOutput").ap()
    csT = nc.dram_tensor("csT", [512, 4, 2], F32, kind="ExternalOutput").ap()

    tr = Tracker()
    es = ExitStack()
    with es:
        def sb(name, shape, dt):
            return es.enter_context(nc.sbuf_tensor(name, shape, dt))

        hbuf = [sb("h0", [128, DC, T], F32), sb("h1", [128, DC, T], F32)]
        ub = sb("ub", [128, DC, T], BF16)
        mTb = sb("mT", [128, DC, T], BF16)
        sq = sb("sq", [128, DC, T], BF16)
        ysb = sb("ysb", [128, DC, T], F32)
        hid = sb("hid", [128, FC, T], BF16)
        NTB = 10
        tb = sb("tbuf", [128, NTB, HW], F32)
        rstd = sb("rstd", [128, T], F32)
        qT = sb("qT", [128, 4, T], BF16)
        kT = sb("kT", [128, 4, 2 * T], BF16)
        Vr = sb("Vr", [128, 8, 512], BF16)
        sS = sb("sS", [128, 4, T + 2], F32)
        sSs = sb("sSs", [128, 4, 4, 18], F32)
        ycT = sb("ycT", [128, 4, T], BF16)
        PT = sb("PT", [128, 2, 1024], BF16)
        PTs = sb("PTs", [128, 5, 128], BF16)
        Rr = sb("Rr", [128, 512], F32)
        oT = sb("oT", [128, 4, T], BF16)
        pTb = [sb("pTb0", [128, 2, T], BF16), sb("pTb1", [128, 2, T], BF16)]
        biasT = sb("biasT", [128, 5, 1024], BF16)
        biasS = sb("biasS", [128, 4, 128], BF16)
        biasN = sb("biasN", [16, 128], BF16)
        ident = sb("identb", [128, 128], BF16)
        onesn = sb("onesn", [128, 128], BF16)
        ones1 = sb("ones1", [128, 64], BF16)
        epsb = sb("epsb", [128, 1], F32)
        gv = sb("gvs", [128, 56], F32)
        gh = sb("ghs", [128, 56], F32)
        cw = sb("cws", [128, 12], F32)
        vnew = PT[0:16, :, :].rearrange("p a (b c) -> p (a b) c", c=512)
        wr = sb("wring", [128, NSLOT, SLAB * 128], BF16)
        ps = es.enter_context(nc.psum_tensor("ps", [128, 8, 512], F32))

        OUT_SEMS = ["oh0", "oh1", "o_cp", "o_cs"] + ["ot%d" % i for i in range(NTB)]
        sem_names = ["pe", "act", "dve", "pool", "hs0", "hs1", "ps0", "ps1", "cst", "cst2", "cinit", "ch0", "ch1"] + \
                    OUT_SEMS + ["w%d" % i for i in range(NSLOT)] + ["wb%d" % i for i in range(NSLOT)] + ["wq%d" % i for i in range(NSLOT)]
        sems = {n: es.enter_context(nc.semaphore("s_" + n)) for n in sem_names}

        bank_rot = [0, 1, 2, 3, 4, 7]
        st = {"bank": 0, "tb": 0, "slab_loaded": 0}
        n_pass = NT + (1 if with_sample else 0)
        total_slabs = n_pass * NSLAB

        def nbank():
            b = bank_rot[st["bank"] % len(bank_rot)]
            st["bank"] += 1
            return b

        def ntb():
            i = st["tb"] % NTB
            st["tb"] += 1
            return i

        def load_slab(g):
            slot = g % NSLOT
            dst = wr[:, slot, :]
            if g < NSLAB:
                src = wall[g]
                tr.dma("pool", lambda e, dst=dst, src=src: [e.dma_start(out=dst, in_=src)],
                       "wq%d" % slot, reads=(), writes=(("w", slot),))
                tr.dma("sp", lambda e, dst=dst, g=g: [e.dma_start(out=wbf[g], in_=dst)],
                       "wb%d" % slot, reads=(("w", slot),), writes=(("wbf", g),))
            else:
                src = wbf[g % NSLAB]
                tr.dma("sp", lambda e, dst=dst, src=src: [e.dma_start(out=dst, in_=src)],
                       "w%d" % slot, reads=(("wbf", g % NSLAB),), writes=(("w", slot),))

        def convert_weights():
            for g in range(NSLAB):
                rd = (("wbf", g - NCV),) if g >= NCV else ()
                tr.dma("pool", lambda e, g=g: [e.dma_start(out=wbf[g], in_=wall[g])],
                       "cv%d" % (g % NCV), reads=rd, writes=(("wbf", g),))

        def ensure_slabs(upto):
            while st["slab_loaded"] <= upto and st["slab_loaded"] < total_slabs:
                load_slab(st["slab_loaded"])
                st["slab_loaded"] += 1

        def wget(gb, expect=None):
            if expect is not None:
                assert BLOCKS[gb % NBLK] == expect, (BLOCKS[gb % NBLK], expect)
            g = gb // SLAB
            assert g < st["slab_loaded"], ("slab not issued", g, st["slab_loaded"])
            slot = g % NSLOT
            off = (gb % SLAB) * 128
            return wr[:, slot, off:off + 128], ("w", slot)

        def wget4(gb, expect0):
            assert gb % 4 == 0 and BLOCKS[gb % NBLK] == expect0
            g = gb // SLAB
            assert g < st["slab_loaded"]
            slot = g % NSLOT
            off = (gb % SLAB) * 128
            return wr[:, slot, off:off + 512], ("w", slot)

        def mm_group(out_ap, out_res, terms):
            n = len(terms)
            for i, (l, lr, r, rr) in enumerate(terms):
                tr.op("pe",
                      lambda e, l=l, r=r, i=i: e.matmul(out_ap, lhsT=l, rhs=r, start=(i == 0),
                                                        stop=(i == n - 1)),
                      reads=(lr, rr), writes=(out_res,), inc=(i == n - 1))

        def act(out, in_, func, reads, writes, scale=1.0, bias=None):
            if bias is None:
                tr.op("act", lambda e: e.activation(out=out, in_=in_, func=func, scale=scale),
                      reads=reads, writes=writes)
            else:
                tr.op("act", lambda e: e.activation(out=out, in_=in_, func=func, scale=scale,
                                                    bias=bias), reads=reads, writes=writes)

        def dve(fn, reads, writes):
            tr.op("dve", fn, reads=reads, writes=writes)

        def out_dma(fn, reads, sem, n=1):
            tr.dma("pool", fn, sem, reads=reads, writes=(), n=n)

        tr.dma("sp", lambda e: [e.dma_start(out=gv[:], in_=gvd), e.dma_start(out=cw[:], in_=cwd)],
               "cst", writes=("gv", "cw"), n=2)
        tr.dma("pool", lambda e: [e.dma_start(out=biasT[:, i, :], in_=biasd[:, i, :]) for i in range(5)]
               + [e.dma_start(out=biasS[:], in_=biassd), e.dma_start(out=biasN[:], in_=biasnd),
                  e.dma_start(out=ident[:], in_=identd)],
               "cinit", writes=("biasT", "biasS", "biasN", "ident"), n=8)
        dve(lambda e: e.tensor_scalar(out=gh[:], in0=gv[:], scalar1=0.5, scalar2=None, op0=ALU.mult),
            reads=("gv",), writes=("gh",))
        dve(lambda e: e.memset(onesn[:], 1.0 / 1024.0), reads=(), writes=("onesn",))
        dve(lambda e: e.memset(ones1[:], 1.0), reads=(), writes=("ones1",))
        dve(lambda e: e.memset(epsb[:], EPS), reads=(), writes=("epsb",))
        dve(lambda e: e.memset(sS[:, :, 0:2], 0.0), reads=(), writes=("sS",))
        dve(lambda e: e.memset(
            biasT[0:64, 0, :].rearrange("p (g q) -> p g q", q=128)[:, :, 64:128], NEG),
            reads=(), writes=("biasT",))
        dve(lambda e: e.memset(
            biasT[64:128, 4, :].rearrange("p (g q) -> p g q", q=128)[:, :, 0:64], NEG),
            reads=(), writes=("biasT",))

        class Half:
            def __init__(self, i, hf, sample=False):
                self.i, self.hf, self.sample = i, hf, sample
                self.W = TS if sample else HW
                self.c0 = 0 if sample else hf * HW
                self.cs = slice(self.c0, self.c0 + self.W)
                self.h = hbuf[i % 2]
                self.hres = "h%d" % (i % 2)
                self.pb = pTb[i % 2]
                self.pres = "p%d" % (i % 2)
                self.gb0 = i * NBLK
                self.last = (not sample) and (i == NT - 1)

            def R(self, name, *idx):
                return (name, self.hf) + idx

            def H(self, c):
                return (self.hres, self.hf, c)

        def stats_all(hv):
            bn = nbank()
            mm_group(ps[:, bn, 0:hv.W], ("ps", bn),
                     [(onesn[:], "onesn", sq[:, c, hv.cs], hv.R("sq", c)) for c in range(DC)])
            ti = ntb()
            act(tb[:, ti, 0:hv.W], ps[:, bn, 0:hv.W], AF.Ln, reads=(("ps", bn), "epsb"),
                writes=(("tb", ti),), bias=epsb[:, 0:1])
            act(rstd[:, hv.cs], tb[:, ti, 0:hv.W], AF.Exp, reads=(("tb", ti),), writes=(hv.R("rstd"),),
                scale=-0.5)

        def u_pre_sq(gi):
            def fn(c):
                def f(hv, gb):
                    act(sq[:, c, hv.cs], hv.h[:, c, hv.cs], AF.Square, reads=(hv.H(c),),
                        writes=(hv.R("sq", c),))
                return f
            return [(0, 0.4, fn(c)) for c in range(DC)]

        def u_op(hs, hv, gi, c, to_m):
            dst, dres = (mTb, "mT") if to_m else (ub, "ub")
            dve(lambda e: e.scalar_tensor_tensor(
                out=dst[:, c, hv.cs], in0=hs.h[:, c, hv.cs], scalar=gv[:, gi * 8 + c:gi * 8 + c + 1],
                in1=rstd[:, hv.cs], op0=ALU.mult, op1=ALU.mult),
                reads=(hs.H(c), "gv", hv.R("rstd")), writes=(hv.R(dres, c),))

        def u_pre_fin(gi, to_m=False):
            def fin(hv, gb):
                stats_all(hv)

            def fn(c):
                def f(hv, gb):
                    u_op(hv, hv, gi, c, to_m)
                return f
            return [(0, 2.3, fin)] + [(0, 0.45, fn(c)) for c in range(DC)]

        def u_post(gi, then_sq, next_sq=False, then_hb=False):
            def first(hv, gb):
                stats_all(hv)

            def fn(c):
                def f(hv, gb):
                    dve(lambda e: e.tensor_tensor(out=ysb[:, c, hv.cs], in0=ysb[:, c, hv.cs],
                                                  in1=rstd[:, hv.cs], op=ALU.mult),
                        reads=(hv.R("ysb", c), hv.R("rstd")), writes=(hv.R("ysb", c),))
                    dve(lambda e: e.tensor_tensor(out=hv.h[:, c, hv.cs], in0=hv.h[:, c, hv.cs],
                                                  in1=ysb[:, c, hv.cs], op=ALU.add),
                        reads=(hv.R("ysb", c), hv.H(c)), writes=(hv.H(c),))
                    if then_sq:
                        act(sq[:, c, hv.cs], hv.h[:, c, hv.cs], AF.Square, reads=(hv.H(c),),
                            writes=(hv.R("sq", c),))
                    if then_hb:
                        act(ub[:, c, hv.cs], hv.h[:, c, hv.cs], AF.Copy, reads=(hv.H(c),),
                            writes=(hv.R("ub", c),))
                    if next_sq and hv.next is not None:
                        hn = hv.next
                        act(sq[:, c, hv.cs], hn.h[:, c, hv.cs], AF.Square, reads=(hn.H(c),),
                            writes=(hv.R("sq", c),))
                return f

            def nfin(hv, gb):
                if hv.next is not None:
                    stats_all(hv)
            us = [(0, 2.5, first)] + [(0, 0.9, fn(c)) for c in range(DC)]
            if next_sq:
                us.append((0, 2.3, nfin))
            return us

        def evac_sq(hv, bn, m, gi, half, doubled=False):
            assert not (half and doubled)
            g_ = gh if (half or doubled) else gv
            act(ysb[:, m, hv.cs], ps[:, bn, 0:hv.W], AF.Identity,
                reads=(("ps", bn), "gh" if (half or doubled) else "gv"), writes=(hv.R("ysb", m),),
                scale=g_[:, gi * 8 + m:gi * 8 + m + 1])
            act(sq[:, m, hv.cs], ps[:, bn, 0:hv.W], AF.Square, reads=(("ps", bn),), writes=(hv.R("sq", m),),
                scale=(0.5 if doubled else 1.0))

        def u_ffn(tag, gpost, from_m=False, with_next=False):
            us = []
            src, sres = (mTb, "mT") if from_m else (ub, "ub")

            def gu(j):
                def f(hv, gb):
                    W = hv.W
                    bn = nbank()
                    terms = []
                    for kc in range(DC):
                        w, wres = wget(gb + kc, (tag + "_gate", kc, j))
                        terms.append((w, wres, src[:, kc, hv.cs], hv.R(sres, kc)))
                    mm_group(ps[:, bn, 0:W], ("ps", bn), terms)
                    terms = []
                    for kc in range(DC):
                        w, wres = wget(gb + 8 + kc, (tag + "_up", kc, j))
                        terms.append((w, wres, src[:, kc, hv.cs], hv.R(sres, kc)))
                    mm_group(ps[:, bn, 256:256 + W], ("ps", bn), terms)
                    ti = ntb()
                    act(tb[:, ti, 0:W], ps[:, bn, 0:W], AF.Silu, reads=(("ps", bn),), writes=(("tb", ti),))
                    dve(lambda e: e.tensor_tensor(out=hid[:, j, hv.cs], in0=tb[:, ti, 0:W],
                                                  in1=ps[:, bn, 256:256 + W], op=ALU.mult),
                        reads=(("tb", ti), ("ps", bn)), writes=(hv.R("hid", j),))
                return f

            def down(m):
                def f(hv, gb):
                    bn = nbank()
                    terms = []
                    for j in range(FC):
                        w, wres = wget(gb + j, (tag + "_down", j, m))
                        terms.append((w, wres, hid[:, j, hv.cs], hv.R("hid", j)))
                    mm_group(ps[:, bn, 0:hv.W], ("ps", bn), terms)
                    evac_sq(hv, bn, m, gpost, True)
                return f
            nxt = []
            if with_next:
                def nsq(c):
                    def f(hv, gb):
                        if hv.next is not None:
                            hn = hv.next
                            act(sq[:, c, hv.cs], hn.h[:, c, hv.cs], AF.Square, reads=(hn.H(c),),
                                writes=(hv.R("sq", c),))
                    return f

                def nfin(hv, gb):
                    if hv.next is not None:
                        stats_all(hv)

                def nun(c):
                    def f(hv, gb):
                        if hv.next is not None:
                            u_op(hv.next, hv, 0, c, True)
                    return f
                nxt = [(0, 0.3, nsq(c)) for c in range(DC)] + [(0, 1.0, nfin)] + \
                      [(0, 0.45, nun(c)) for c in range(DC)]
            for j in range(FC):
                us.append((16, None, gu(j)))
                if j >= 2 and nxt:
                    us.append(nxt.pop(0))
            assert not nxt
            for m in range(DC):
                us.append((22, None, down(m)))
            return us

        def dense8(hv, gb, m):
            bn = nbank()
            terms = []
            for kc in range(DC):
                w, wres = wget(gb + kc, ("w_in", kc, m))
                terms.append((w, wres, ub[:, kc, hv.cs], hv.R("ub", kc)))
            mm_group(ps[:, bn, 0:hv.W], ("ps", bn), terms)
            return bn

        def u_inproj_prompt():
            us = []

            def q(m):
                def f(hv, gb):
                    bn = dense8(hv, gb, m)
                    act(qT[:, m, hv.cs], ps[:, bn, 0:hv.W], AF.Copy, reads=(("ps", bn),),
                        writes=(hv.R("qT", m),), scale=0.125)
                return f

            def k(m):
                def f(hv, gb):
                    bn = dense8(hv, gb, 4 + m)
                    gblk = 4 * hv.i + 2 * hv.hf
                    col = (gblk % 8) * 128
                    act(kT[:, m, col:col + hv.W], ps[:, bn, 0:hv.W], AF.Copy, reads=(("ps", bn),),
                        writes=(("kT", gblk % 8, m), ("kT", (gblk + 1) % 8, m)))
                    if hv.last:
                        ti = ntb()
                        dve(lambda e: e.tensor_copy(out=tb[:, ti, :], in_=ps[:, bn, 0:hv.W]),
                            reads=(("ps", bn),), writes=(("tb", ti),))
                        out_dma(lambda e: [e.dma_start(out=kpT[m * 128:(m + 1) * 128, hv.cs], in_=tb[:, ti, :])],
                                reads=(("tb", ti),), sem="ot%d" % ti)
                return f

            def v(hv, gb):
                wv = [wget4(gb + 4 * kc, ("w_in", kc, 8)) for kc in range(DC)]
                for t2 in range(2):
                    bn = nbank()
                    c0 = hv.c0 + t2 * 128
                    mm_group(ps[:, bn, :], ("ps", bn),
                             [(ub[:, kc, c0:c0 + 128], hv.R("ub", kc), wv[kc][0], wv[kc][1])
                              for kc in range(DC)])
                    slot = (4 * hv.i + 2 * hv.hf + t2) % 8
                    dve(lambda e, bn=bn, slot=slot: e.tensor_copy(out=Vr[:, slot, :], in_=ps[:, bn, :]),
                        reads=(("ps", bn),), writes=(("V", slot),))
                    if hv.last:
                        for hh in range(2):
                            ti = ntb()
                            act(tb[:, ti, :], ps[:, bn, hh * 256:(hh + 1) * 256], AF.Copy, reads=(("ps", bn),),
                                writes=(("tb", ti),))
                            r0 = (2 * hv.hf + t2) * 128
                            out_dma(lambda e, ti=ti, r0=r0, hh=hh: [e.dma_start(
                                out=vp[r0:r0 + 128, hh * 256:(hh + 1) * 256], in_=tb[:, ti, :])],
                                reads=(("tb", ti),), sem="ot%d" % ti)

            def x(m):
                def f(hv, gb):
                    bn = dense8(hv, gb, 12 + m)
                    act(ysb[:, m, hv.cs], ps[:, bn, 0:hv.W], AF.Copy, reads=(("ps", bn),),
                        writes=(hv.R("ysb", m),))
                return f

            def b(m):
                def f(hv, gb):
                    bn = dense8(hv, gb, 16 + m)
                    act(ysb[:, 4 + m, hv.cs], ps[:, bn, 0:hv.W], AF.Copy, reads=(("ps", bn),),
                        writes=(hv.R("ysb", 4 + m),))
                return f

            def c(m):
                def f(hv, gb):
                    bn = dense8(hv, gb, 20 + m)
                    dve(lambda e: e.tensor_tensor(out=sS[:, m, 2 + hv.c0:2 + hv.c0 + hv.W],
                                                  in0=ysb[:, m, hv.cs], in1=ps[:, bn, 0:hv.W], op=ALU.mult),
                        reads=(hv.R("ysb", m), ("ps", bn)), writes=("sS",))
                return f
            for m in range(4):
                us.append((8, None, q(m)))
            for m in range(4):
                us.append((8, None, k(m)))
            us.append((32, 4.5, v))
            for m in range(4):
                us.append((8, None, x(m)))
            for m in range(4):
                us.append((8, None, b(m)))
            for m in range(4):
                us.append((8, None, c(m)))
            return us

        def u_conv_prompt():
            def cv(c):
                def f(hv, gb):
                    ti = ntb()
                    o = hv.c0
                    W = hv.W
                    dve(lambda e: e.tensor_scalar(out=tb[:, ti, :], in0=sS[:, c, o:o + W],
                                                  scalar1=cw[:, c:c + 1], scalar2=None, op0=ALU.mult),
                        reads=("sS", "cw"), writes=(("tb", ti),))
                    dve(lambda e: e.scalar_tensor_tensor(out=tb[:, ti, :], in0=sS[:, c, o + 1:o + 1 + W],
                                                         scalar=cw[:, 4 + c:5 + c], in1=tb[:, ti, :],
                                                         op0=ALU.mult, op1=ALU.add),
                        reads=("sS", "cw", ("tb", ti)), writes=(("tb", ti),))
                    dve(lambda e: e.scalar_tensor_tensor(out=tb[:, ti, :], in0=sS[:, c, o + 2:o + 2 + W],
                                                         scalar=cw[:, 8 + c:9 + c], in1=tb[:, ti, :],
                                                         op0=ALU.mult, op1=ALU.add),
                        reads=("sS", "cw", ("tb", ti)), writes=(("tb", ti),))
                    dve(lambda e: e.tensor_tensor(out=ycT[:, c, hv.cs], in0=tb[:, ti, :],
                                                  in1=ysb[:, 4 + c, hv.cs], op=ALU.mult),
                        reads=(("tb", ti), hv.R("ysb", 4 + c)), writes=(hv.R("ycT", c),))
                return f

            def fin(hv, gb):
                if hv.hf == 1:
                    if hv.last:
                        out_dma(lambda e: [e.dma_start(out=cpT.rearrange("(c p) j -> p c j", p=128),
                                                       in_=sS[:, :, T:T + 2])], reads=("sS",), sem="o_cp")
                    else:
                        dve(lambda e: e.tensor_copy(out=sS[:, :, 0:2], in_=sS[:, :, T:T + 2]),
                            reads=("sS",), writes=("sS",))
            return [(0, 1.7, cv(c)) for c in range(4)] + [(0, 0.2, fin)]

        def u_attn_prompt():
            us = []

            def mk(qi, ii_pos):
                def f(hv, gb):
                    stt = hv.att
                    steps = stt["steps"]
                    sidx = stt["next"]
                    stt["next"] += 1
                    if sidx >= len(steps):
                        return
                    if sidx == 0:
                        emit_scores(hv, 0)
                    if sidx + 1 < len(steps):
                        emit_scores(hv, sidx + 1)
                    emit_pv(hv, sidx)
                return f
            for k_ in range(10):
                us.append((0, 1.9, mk(0, 0), "att_first" if k_ == 0 else ("att_last" if k_ == 9 else "att")))
            return us

        def att_setup(hv):
            steps = []
            for qq in range(2):
                qb = 2 * hv.hf + qq
                G = 4 * hv.i + qb
                iis = [ii for ii in range(5) if G - 4 + ii >= 0]
                for n_, ii in enumerate(iis):
                    steps.append((qb, G, ii, n_ == 0, n_ == len(iis) - 1))
            hv.att = {"steps": steps, "next": 0}

        def emit_scores(hv, sidx):
            qb, G, ii, first, lastk = hv.att["steps"][sidx]
            gk = G - 4 + ii
            kcol = (gk % 8) * 128
            sb0 = 0 if (sidx % 2 == 0) else 2
            pp = sidx % 2
            for par in range(2):
                bn = sb0 + par
                tr.op("pe", lambda e, bn=bn, par=par: e.matmul(
                    ps[:, bn, :], lhsT=ident[:], rhs=biasT[:, ii, par * 512:(par + 1) * 512],
                    start=True, stop=False, skip_group_check=True),
                    reads=("ident", "biasT"), writes=(("ps", bn),), inc=False)
            for hc in range(4):
                for par in range(2):
                    bn = sb0 + par
                    tr.op("pe", lambda e, bn=bn, par=par, hc=hc: e.matmul(
                        ps[:, bn, hc * 128:(hc + 1) * 128],
                        lhsT=kT[par * 64:(par + 1) * 64, hc, kcol:kcol + 128],
                        rhs=qT[par * 64:(par + 1) * 64, hc, qb * 128:(qb + 1) * 128],
                        start=False, stop=True, skip_group_check=True),
                        reads=(("kT", gk % 8, hc), hv.R("qT", hc)), writes=(("ps", bn),),
                        inc=(hc == 3 and par == 1))
            act(PT[:, pp, :].rearrange("p (a b) -> p a b", a=2), ps[:, sb0:sb0 + 2, :], AF.Exp,
                reads=(("ps", sb0), ("ps", sb0 + 1)), writes=(("PT", pp),))

        def emit_pv(hv, sidx):
            qb, G, ii, first, lastk = hv.att["steps"][sidx]
            gk = G - 4 + ii
            vslot = gk % 8
            pp = sidx % 2
            bs_, bo_ = 5, 6
            for par in range(2):
                tr.op("pe", lambda e, par=par: e.matmul(
                    ps[par * 64:(par + 1) * 64, bs_, :], lhsT=ones1[:, 0:64],
                    rhs=PT[:, pp, par * 512:(par + 1) * 512], start=first, stop=lastk),
                    reads=("ones1", ("PT", pp)), writes=(("ps", bs_),), inc=False)
            for h_ in range(8):
                par, hc = h_ % 2, h_ // 2
                tr.op("pe", lambda e, par=par, hc=hc, h_=h_: e.matmul(
                    ps[par * 64:(par + 1) * 64, bo_, hc * 128:(hc + 1) * 128],
                    lhsT=Vr[:, vslot, h_ * 64:(h_ + 1) * 64],
                    rhs=PT[:, pp, par * 512 + hc * 128:par * 512 + (hc + 1) * 128],
                    start=(first and hc == 0), stop=lastk, skip_group_check=True),
                    reads=(("V", vslot), ("PT", pp)), writes=(("ps", bo_),), inc=(h_ == 7))
            if lastk:
                act(Rr[:], ps[:, bs_, :], AF.Ln, reads=(("ps", bs_),), writes=("Rr",))
                act(Rr[:], Rr[:], AF.Exp, reads=("Rr",), writes=("Rr",), scale=-1.0)
                dve(lambda e: e.tensor_tensor(
                    out=oT[:, :, qb * 128:(qb + 1) * 128],
                    in0=ps[:, bo_, :].rearrange("p (c q) -> p c q", q=128),
                    in1=Rr[:].rearrange("p (c q) -> p c q", q=128), op=ALU.mult),
                    reads=(("ps", bo_), "Rr"), writes=tuple(hv.R("oT", c) for c in range(4)))

        def u_gating():
            us = []

            def gate(m):
                def f(hv, gb):
                    W = hv.W
                    b1, b2 = nbank(), nbank()
                    terms = []
                    for kc in range(4):
                        w, wres = wget(gb + kc, ("w_att_out", kc, m))
                        terms.append((w, wres, oT[:, kc, hv.cs], hv.R("oT", kc)))
                    mm_group(ps[:, b1, 0:W], ("ps", b1), terms)
                    terms = []
                    for kc in range(4):
                        w, wres = wget(gb + 4 + kc, ("w_conv_out", kc, m))
                        terms.append((w, wres, ycT[:, kc, hv.cs], hv.R("ycT", kc)))
                    mm_group(ps[:, b1, 256:256 + W], ("ps", b1), terms)
                    terms = []
                    for kc in range(DC):
                        w, wres = wget(gb + 8 + kc, ("w_in", kc, 24 + m))
                        terms.append((w, wres, ub[:, kc, hv.cs], hv.R("ub", kc)))
                    mm_group(ps[:, b2, 0:W], ("ps", b2), terms)
                    terms = []
                    for kc in range(DC):
                        w, wres = wget(gb + 16 + kc, ("w_in", kc, 32 + m))
                        terms.append((w, wres, ub[:, kc, hv.cs], hv.R("ub", kc)))
                    mm_group(ps[:, b2, 256:256 + W], ("ps", b2), terms)
                    t1, t2 = ntb(), ntb()
                    act(tb[:, t1, 0:W], ps[:, b2, 0:W], AF.Tanh, reads=(("ps", b2),), writes=(("tb", t1),),
                        scale=0.5)
                    act(tb[:, t2, 0:W], ps[:, b2, 256:256 + W], AF.Tanh, reads=(("ps", b2),),
                        writes=(("tb", t2),), scale=0.5)
                    dve(lambda e: e.scalar_tensor_tensor(out=tb[:, t1, 0:W], in0=tb[:, t1, 0:W], scalar=1.0,
                                                         in1=ps[:, b1, 0:W], op0=ALU.add, op1=ALU.mult),
                        reads=(("tb", t1), ("ps", b1)), writes=(("tb", t1),))
                    dve(lambda e: e.scalar_tensor_tensor(out=tb[:, t2, 0:W], in0=tb[:, t2, 0:W], scalar=1.0,
                                                         in1=ps[:, b1, 256:256 + W], op0=ALU.add, op1=ALU.mult),
                        reads=(("tb", t2), ("ps", b1)), writes=(("tb", t2),))
                    dve(lambda e: e.tensor_tensor(out=mTb[:, m, hv.cs], in0=tb[:, t1, 0:W],
                                                  in1=tb[:, t2, 0:W], op=ALU.add),
                        reads=(("tb", t1), ("tb", t2)), writes=(hv.R("mT", m),))
                return f

            def outp(m):
                def f(hv, gb):
                    bn = nbank()
                    terms = []
                    for kc in range(DC):
                        w, wres = wget(gb + kc, ("w_out", kc, m))
                        terms.append((w, wres, mTb[:, kc, hv.cs], hv.R("mT", kc)))
                    mm_group(ps[:, bn, 0:hv.W], ("ps", bn), terms)
                    evac_sq(hv, bn, m, 3, False, doubled=True)
                return f
            for m in range(DC):
                us.append((24, None, gate(m)))
            for m in range(DC):
                us.append((8, None, outp(m)))
            return us

        def u_ple():
            us = []

            def hb(c):
                def f(hv, gb):
                    act(ub[:, c, hv.cs], hv.h[:, c, hv.cs], AF.Copy, reads=(hv.H(c),), writes=(hv.R("ub", c),))
                return f

            def pl(m):
                def f(hv, gb):
                    W = hv.W
                    bn = nbank()
                    terms = []
                    for kc in range(DC):
                        w, wres = wget(gb + kc, ("w_ple_gate", kc, m))
                        terms.append((w, wres, ub[:, kc, hv.cs], hv.R("ub", kc)))
                    mm_group(ps[:, bn, 0:W], ("ps", bn), terms)
                    terms = []
                    for kc in range(2):
                        w, wres = wget(gb + 8 + kc, ("w_ple_proj", kc, m))
                        terms.append((w, wres, hv.pb[:, kc, hv.cs], hv.pres))
                    mm_group(ps[:, bn, 256:256 + W], ("ps", bn), terms)
                    ti = ntb()
                    act(tb[:, ti, 0:W], ps[:, bn, 0:W], AF.Tanh, reads=(("ps", bn),), writes=(("tb", ti),),
                        scale=0.5)
                    dve(lambda e: e.scalar_tensor_tensor(out=ysb[:, m, hv.cs], in0=tb[:, ti, 0:W], scalar=1.0,
                                                         in1=ps[:, bn, 256:256 + W], op0=ALU.add, op1=ALU.mult),
                        reads=(("tb", ti), ("ps", bn)), writes=(hv.R("ysb", m),))
                    act(sq[:, m, hv.cs], ysb[:, m, hv.cs], AF.Square, reads=(hv.R("ysb", m),),
                        writes=(hv.R("sq", m),), scale=0.5)
                    act(ysb[:, m, hv.cs], ysb[:, m, hv.cs], AF.Identity, reads=(hv.R("ysb", m), "gh"),
                        writes=(hv.R("ysb", m),), scale=gh[:, 6 * 8 + m:6 * 8 + m + 1])
                return f
            def un(c):
                def f(hv, gb):
                    if hv.next is not None:
                        u_op(hv.next, hv, 0, c, True)
                return f
            for m in range(DC):
                us.append((10, None, pl(m)))
            return us

        def u_output():
            def f(hv, gb):
                hres = hv.hres
                if hv.sample:
                    out_dma(lambda e: [e.dma_start(out=ysT.rearrange("(c p) t -> p c t", p=128),
                                                   in_=hv.h[:, :, 0:TS])],
                            reads=tuple(hv.H(c) for c in range(DC)), sem="oh%d" % (hv.i % 2))
                else:
                    t0 = hv.i * T + hv.c0
                    dst = yT.rearrange("(c p) t -> p c t", p=128)[:, :, t0:t0 + hv.W]
                    out_dma(lambda e: [e.dma_start(out=dst, in_=hv.h[:, :, hv.cs])],
                            reads=tuple(hv.H(c) for c in range(DC)), sem="oh%d" % (hv.i % 2))
            return [(0, 0.1, f, "output")]

        def u_inproj_sample():
            us = []

            def pre(hv, gb):
                load_cache(0)
                load_cache(1)
                tr.dma("sp", lambda e: [e.dma_start(out=sSs[:, :, :, 0:2], in_=ccTd)], "cst2",
                       writes=("sSs",))

            def q(m):
                def f(hv, gb):
                    bn = dense8(hv, gb, m)
                    act(qT[:, m, hv.cs], ps[:, bn, 0:hv.W], AF.Copy, reads=(("ps", bn),),
                        writes=(hv.R("qT", m),), scale=0.125)
                return f

            def k(m):
                def f(hv, gb):
                    bn = dense8(hv, gb, 4 + m)
                    act(mTb[:, m, 0:TS], ps[:, bn, 0:TS], AF.Copy, reads=(("ps", bn),), writes=(hv.R("mT", m),))
                    ti = ntb()
                    dve(lambda e: e.tensor_copy(out=tb[:, ti, 0:TS], in_=ps[:, bn, 0:TS]),
                        reads=(("ps", bn),), writes=(("tb", ti),))
                    out_dma(lambda e: [e.dma_start(out=ksT[m * 128:(m + 1) * 128, :], in_=tb[:, ti, 0:TS])],
                            reads=(("tb", ti),), sem="ot%d" % ti)
                return f

            def v(hv, gb):
                wv = [wget4(gb + 4 * kc, ("w_in", kc, 8)) for kc in range(DC)]
                for b in range(4):
                    bn = nbank()
                    mm_group(ps[0:16, bn, :], ("ps", bn),
                             [(ub[:, kc, b * 16:(b + 1) * 16], hv.R("ub", kc), wv[kc][0], wv[kc][1])
                              for kc in range(DC)])
                    dve(lambda e, b=b, bn=bn: e.tensor_copy(out=vnew[:, b, :], in_=ps[0:16, bn, :]),
                        reads=(("ps", bn),), writes=("vnew",))
                    for hh in range(2):
                        ti = ntb()
                        act(tb[0:16, ti, :], ps[0:16, bn, hh * 256:(hh + 1) * 256], AF.Copy,
                            reads=(("ps", bn),), writes=(("tb", ti),))
                        out_dma(lambda e, b=b, ti=ti, hh=hh: [e.dma_start(
                            out=vs[b, :, hh * 256:(hh + 1) * 256], in_=tb[0:16, ti, :])],
                            reads=(("tb", ti),), sem="ot%d" % ti)

            def x(m):
                def f(hv, gb):
                    bn = dense8(hv, gb, 12 + m)
                    act(ysb[:, m, 0:TS], ps[:, bn, 0:TS], AF.Copy, reads=(("ps", bn),), writes=(hv.R("ysb", m),))
                return f

            def b_(m):
                def f(hv, gb):
                    bn = dense8(hv, gb, 16 + m)
                    act(ysb[:, 4 + m, 0:TS], ps[:, bn, 0:TS], AF.Copy, reads=(("ps", bn),),
                        writes=(hv.R("ysb", 4 + m),))
                return f

            def c(m):
                def f(hv, gb):
                    bn = dense8(hv, gb, 20 + m)
                    dve(lambda e: e.tensor_tensor(
                        out=sSs[:, m, :, 2:18],
                        in0=ysb[:, m, 0:TS].rearrange("p (b t) -> p b t", t=16),
                        in1=ps[:, bn, 0:TS].rearrange("p (b t) -> p b t", t=16), op=ALU.mult),
                        reads=(hv.R("ysb", m), ("ps", bn)), writes=("sSs",))
                return f
            us.append((0, 0.1, pre))
            for m in range(4):
                us.append((8, None, q(m)))
            for m in range(4):
                us.append((8, None, k(m)))
            us.append((32, 4.5, v))
            for m in range(4):
                us.append((8, None, x(m)))
            for m in range(4):
                us.append((8, None, b_(m)))
            for m in range(4):
                us.append((8, None, c(m)))
            return us

        def load_cache(b):
            hf = b % 2
            tr.dma("pool", lambda e, b=b, hf=hf: [
                e.dma_start(out=kT[:, :, hf * 512:(hf + 1) * 512], in_=ckTd[b]),
                e.dma_start(out=Vr[:, hf * 4:(hf + 1) * 4, :],
                            in_=cvd[b].rearrange("(k p) f -> p k f", p=128))],
                "ch%d" % hf, writes=tuple(("kT", hf * 4 + k4, m) for m in range(4) for k4 in range(4)) +
                tuple(("V", hf * 4 + k) for k in range(4)), n=2)

        def u_attn_sample():
            def seq(b):
                def f(hv, gb):
                    kTs = mTb
                    hf = b % 2
                    bs_, bo_ = nbank(), nbank()
                    for blk in range(5):
                        KP = 128 if blk < 4 else 16
                        for par in range(2):
                            bn = nbank()
                            if blk < 4:
                                tr.op("pe", lambda e, bn=bn, blk=blk, par=par: e.matmul(
                                    ps[:, bn, 0:64], lhsT=ident[:], rhs=biasS[:, blk, par * 64:(par + 1) * 64],
                                    start=True, stop=False, skip_group_check=True),
                                    reads=("ident", "biasS"), writes=(("ps", bn),), inc=False)
                            else:
                                tr.op("pe", lambda e, bn=bn, par=par: e.matmul(
                                    ps[0:16, bn, 0:64], lhsT=ident[0:16, 0:16],
                                    rhs=biasN[:, par * 64:(par + 1) * 64], start=True, stop=False,
                                    skip_group_check=True),
                                    reads=("ident", "biasN"), writes=(("ps", bn),), inc=False)
                            for hc in range(4):
                                if blk < 4:
                                    l = kT[par * 64:(par + 1) * 64, hc,
                                           hf * 512 + blk * 128:hf * 512 + (blk + 1) * 128]
                                    lres = ("kT", hf * 4 + blk, hc)
                                else:
                                    l = kTs[par * 64:(par + 1) * 64, hc, b * 16:(b + 1) * 16]
                                    lres = hv.R("mT", hc)
                                tr.op("pe", lambda e, bn=bn, l=l, par=par, hc=hc, KP=KP: e.matmul(
                                    ps[0:KP, bn, hc * 16:(hc + 1) * 16], lhsT=l,
                                    rhs=qT[par * 64:(par + 1) * 64, hc, b * 16:(b + 1) * 16],
                                    start=False, stop=True, skip_group_check=True),
                                    reads=(lres, hv.R("qT", hc)), writes=(("ps", bn),), inc=(hc == 3))
                            act(PTs[0:KP, blk, par * 64:(par + 1) * 64], ps[0:KP, bn, 0:64], AF.Exp,
                                reads=(("ps", bn),), writes=(("PTs", blk),))
                    for blk in range(5):
                        KP = 128 if blk < 4 else 16
                        for par in range(2):
                            tr.op("pe", lambda e, par=par, blk=blk, KP=KP: e.matmul(
                                ps[par * 64:(par + 1) * 64, bs_, 0:64], lhsT=ones1[0:KP, 0:64],
                                rhs=PTs[0:KP, blk, par * 64:(par + 1) * 64], start=(blk == 0), stop=(blk == 4)),
                                reads=("ones1", ("PTs", blk)), writes=(("ps", bs_),), inc=False)
                        for h_ in range(8):
                            par, hc = h_ % 2, h_ // 2
                            if blk < 4:
                                l = Vr[:, hf * 4 + blk, h_ * 64:(h_ + 1) * 64]
                                lres = ("V", hf * 4 + blk)
                            else:
                                l = vnew[0:16, b, h_ * 64:(h_ + 1) * 64]
                                lres = "vnew"
                            tr.op("pe", lambda e, l=l, par=par, hc=hc, blk=blk, KP=KP: e.matmul(
                                ps[par * 64:(par + 1) * 64, bo_, hc * 16:(hc + 1) * 16], lhsT=l,
                                rhs=PTs[0:KP, blk, par * 64 + hc * 16:par * 64 + (hc + 1) * 16],
                                start=(blk == 0 and hc == 0), stop=(blk == 4), skip_group_check=True),
                                reads=(lres, ("PTs", blk)), writes=(("ps", bo_),), inc=(h_ == 7))
                    act(Rr[:, 0:64], ps[:, bs_, 0:64], AF.Ln, reads=(("ps", bs_),), writes=("Rr",))
                    act(Rr[:, 0:64], Rr[:, 0:64], AF.Exp, reads=("Rr",), writes=("Rr",), scale=-1.0)
                    dve(lambda e: e.tensor_tensor(
                        out=oT[:, :, b * 16:(b + 1) * 16],
                        in0=ps[:, bo_, 0:64].rearrange("p (c q) -> p c q", q=16),
                        in1=Rr[:, 0:64].rearrange("p (c q) -> p c q", q=16), op=ALU.mult),
                        reads=(("ps", bo_), "Rr"), writes=tuple(hv.R("oT", c) for c in range(4)))
                    if b + 2 < 4:
                        load_cache(b + 2)
                return f

            def cv(c):
                def f(hv, gb):
                    ti = ntb()
                    tv = tb[:, ti, 0:TS].rearrange("p (b t) -> p b t", t=16)
                    dve(lambda e: e.tensor_scalar(out=tv, in0=sSs[:, c, :, 0:16], scalar1=cw[:, c:c + 1],
                                                  scalar2=None, op0=ALU.mult),
                        reads=("sSs", "cw"), writes=(("tb", ti),))
                    dve(lambda e: e.scalar_tensor_tensor(out=tv, in0=sSs[:, c, :, 1:17],
                                                         scalar=cw[:, 4 + c:5 + c], in1=tv,
                                                         op0=ALU.mult, op1=ALU.add),
                        reads=("sSs", "cw", ("tb", ti)), writes=(("tb", ti),))
                    dve(lambda e: e.scalar_tensor_tensor(out=tv, in0=sSs[:, c, :, 2:18],
                                                         scalar=cw[:, 8 + c:9 + c], in1=tv,
                                                         op0=ALU.mult, op1=ALU.add),
                        reads=("sSs", "cw", ("tb", ti)), writes=(("tb", ti),))
                    dve(lambda e: e.tensor_tensor(out=ycT[:, c, 0:TS], in0=tb[:, ti, 0:TS],
                                                  in1=ysb[:, 4 + c, 0:TS], op=ALU.mult),
                        reads=(("tb", ti), hv.R("ysb", 4 + c)), writes=(hv.R("ycT", c),))
                return f

            def fin(hv, gb):
                out_dma(lambda e: [e.dma_start(out=csT[c * 128:(c + 1) * 128], in_=sSs[:, c, :, 16:18])
                                   for c in range(4)], reads=("sSs",), sem="o_cs", n=4)
            return [(0, 3.0, seq(b)) for b in range(4)] + [(0, 0.5, cv(c)) for c in range(4)] + [(0, 0.1, fin)]

        def pass_units(sample):
            us = []
            us += u_ffn("w1", 1, from_m=True)
            us += u_post(1, True) + u_pre_fin(2)
            if sample:
                us += u_inproj_sample() + u_attn_sample()
            else:
                us += u_inproj_prompt() + u_conv_prompt() + u_attn_prompt()
            us += u_gating()
            us += u_post(3, True) + u_pre_fin(4)
            us += u_ffn("w2", 5, with_next=True)
            us += u_post(5, False, then_hb=True)
            us += u_ple()
            us += u_post(6, False)
            us += u_output()
            return us

        PROLOGUE = u_pre_sq(0) + u_pre_fin(0, to_m=True)
        PU = pass_units(False)
        PUS = pass_units(True)
        assert sum(u[0] for u in PU) == NBLK and sum(u[0] for u in PUS) == NBLK, \
            (sum(u[0] for u in PU), sum(u[0] for u in PUS), NBLK)

        def ucost(u):
            return u[1] if u[1] is not None else u[0] * 0.135

        def load_tile(i):
            hb = hbuf[i % 2]
            hres = "h%d" % (i % 2)
            src = xT.rearrange("(c p) t -> p c t", p=128)[:, :, i * T:(i + 1) * T]
            tr.dma("pool", lambda e: [e.dma_start(out=hb[:], in_=src)], "hs%d" % (i % 2),
                   writes=tuple((hres, hf, c) for c in range(DC) for hf in range(2)))
            psrc = pT.rearrange("(c p) t -> p c t", p=128)[:, :, i * T:(i + 1) * T]
            pb = pTb[i % 2]
            tr.dma("pool", lambda e: [e.dma_start(out=pb[:], in_=psrc)], "ps%d" % (i % 2),
                   writes=("p%d" % (i % 2),))

        def load_sample(i):
            hb = hbuf[i % 2]
            hres = "h%d" % (i % 2)
            src = xsT.rearrange("(c p) t -> p c t", p=128)
            tr.dma("pool", lambda e: [e.dma_start(out=hb[:, :, 0:TS], in_=src)], "hs%d" % (i % 2),
                   writes=tuple((hres, hf, c) for c in range(DC) for hf in range(2)))
            psrc = psT.rearrange("(c p) t -> p c t", p=128)
            pb = pTb[i % 2]
            tr.dma("pool", lambda e: [e.dma_start(out=pb[:, :, 0:TS], in_=psrc)], "ps%d" % (i % 2),
                   writes=("p%d" % (i % 2),))

        NTAIL = 10

        def make_stream(hf):
            sl = []
            hvs = [Half(i, hf) for i in range(NT)]
            for i in range(NT):
                hvs[i].next = hvs[i + 1] if i + 1 < NT else None
            pending = []
            for i in range(NT):
                hv = hvs[i]
                gb = hv.gb0
                if i == 0:
                    for u in PROLOGUE:
                        sl.append((hv, u, gb))
                main, tailu = PU[:-NTAIL], PU[-NTAIL:]
                for k_, u in enumerate(main):
                    sl.append((hv, u, gb))
                    gb += u[0]
                    if pending and u[0] > 0:
                        sl.append(pending.pop(0))
                assert not pending
                assert all(u[0] == 0 for u in tailu)
                pending = [(hv, u, gb) for u in tailu]
            sl += pending
            return sl
        SA, SB = make_stream(0), make_stream(1)
        if with_sample:
            hvS = Half(NT, 0, sample=True)
            gb = hvS.gb0
            hvS.next = None
            SS = [(hvS, u, gb) for u in PROLOGUE]
            for u in PUS:
                SS.append((hvS, u, gb))
                gb += u[0]
        else:
            SS = []

        def load_index(i):
            if i < NT:
                load_tile(i)
            elif i == NT and with_sample:
                load_sample(i)

        load_index(0)
        load_index(1)

        def emit(entry):
            hv, u, gb = entry
            if not hv.sample and not hasattr(hv, "att"):
                att_setup(hv)
            u[2](hv, gb)
            if len(u) > 3 and u[3] == "output" and not hv.sample and hv.hf == 1:
                load_index(hv.i + 2)

        ia = ib = 0
        ta = tb_ = 0.0
        LEAD = 12.0
        in_att = [None]

        def kind(ent):
            return ent[1][3] if len(ent[1]) > 3 else ""

        def note(which, ent):
            k = kind(ent)
            if k == "att_first":
                in_att[0] = which
            elif k == "att_last":
                in_att[0] = None
        while ia < len(SA) or ib < len(SB):
            gB = (SB[ib][2] // SLAB) if ib < len(SB) else (total_slabs if not SS else NT * NSLAB)
            ensure_slabs(gB + NSLOT - 1)
            pickA = False
            if ia < len(SA):
                ua = SA[ia]
                lastslab = (ua[2] + max(ua[1][0], 1) - 1) // SLAB
                fits = lastslab <= gB + NSLOT - 1
                if ib >= len(SB):
                    assert fits
                    pickA = True
                elif fits and (ta - tb_) < LEAD and lastslab <= gB + NSLOT - 3:
                    pickA = True
            if in_att[0] == "A":
                pickA = True
            elif in_att[0] == "B":
                pickA = False
            if pickA:
                note("A", SA[ia])
                emit(SA[ia])
                ta += ucost(SA[ia][1])
                ia += 1
            else:
                note("B", SB[ib])
                emit(SB[ib])
                tb_ += ucost(SB[ib][1])
                ib += 1
        for k_, ent in enumerate(SS):
            gS = ent[2] // SLAB
            ensure_slabs(gS + NSLOT - 1)
            emit(ent)
        tr.final_wait("sp", OUT_SEMS)
        for e_ in tr.CE:
            assert not tr.dangling[e_], e_

        def replay(name, eng):
            for item in tr.prog[name]:
                if item[0] == "wait":
                    eng.wait_ge(sems[item[1]], item[2])
                elif item[0] == "op":
                    ins = item[1](eng)
                    if item[2] is not None:
                        ins.then_inc(sems[item[2]], 1)
                else:
                    for ins in item[1](eng):
                        ins.then_inc(sems[item[2]], 16)

        with nc.Block() as block:
            @block.tensor
            def _(e):
                replay("pe", e)

            @block.scalar
            def _(e):
                replay("act", e)

            @block.vector
            def _(e):
                replay("dve", e)

            @block.gpsimd
            def _(e):
                replay("pool", e)

            @block.sync
            def _(e):
                replay("sp", e)
    return nc


def _weights_stream(W):
    wall = np.empty((NSLAB, 128, SLAB * 128), np.float32)
    for b, (name, kc, mc) in enumerate(BLOCKS):
        g, o = divmod(b, SLAB)
        wall[g, :, o * 128:(o + 1) * 128] = W[name][kc * 128:(kc + 1) * 128, mc * 128:(mc + 1) * 128]
    return wall


def _bias_tiles(table):
    kk = np.arange(128)[:, None]
    qq = np.arange(128)[None, :]
    bp = np.empty((128, 5, 1024), np.float32)
    for ii in range(5):
        d = np.clip(128 * (4 - ii) + qq - kk, -128, 128) + 128
        for h in range(8):
            par, hc = h % 2, h // 2
            bp[:, ii, par * 512 + hc * 128:par * 512 + (hc + 1) * 128] = table[h][d]
    tt = np.arange(16)[None, :]
    bs = np.empty((128, 4, 128), np.float32)
    for blk in range(4):
        d = np.clip(512 + tt - (128 * blk + kk), -128, 128) + 128
        for h in range(8):
            par, hc = h % 2, h // 2
            bs[:, blk, par * 64 + hc * 16:par * 64 + (hc + 1) * 16] = table[h][d]
    bn = np.empty((16, 128), np.float32)
    d = np.clip(tt - np.arange(16)[:, None], -128, 128) + 128
    for h in range(8):
        par, hc = h % 2, h // 2
        bn[:, par * 64 + hc * 16:par * 64 + (hc + 1) * 16] = table[h][d]
    return bp, bs, bn


_NC_CACHE = {}


def _prep_inputs(inp, NT):
    f = lambda a: np.asarray(a, dtype=np.float32)
    W = {k: f(inp[k])[0] for k in ("w1_gate", "w1_up", "w1_down", "w_in", "w_att_out", "w_conv_out",
                                   "w_out", "w2_gate", "w2_up", "w2_down", "w_ple_gate", "w_ple_proj")}
    wall = _weights_stream(W)
    g = f(inp["norm_g"])[0]
    gv = np.ascontiguousarray(g.reshape(7, 8, 128).transpose(2, 0, 1).reshape(128, 56))
    cwv = f(inp["conv_w"])[0]
    cw = np.ascontiguousarray(cwv.reshape(3, 4, 128).transpose(2, 0, 1).reshape(128, 12))
    bp, bs, bn = _bias_tiles(f(inp["rel_bias"])[0])
    ident = np.eye(128, dtype=np.float32)
    xp, xs = f(inp["x_prompt"]), f(inp["x_sample"])
    pp, psm = f(inp["p_prompt"])[0], f(inp["p_sample"])[0]
    ck, cv, cc = f(inp["cache_k"])[0], f(inp["cache_v"])[0], f(inp["cache_conv"])[0]
    SP = NT * T
    maps = []
    for c in range(NCORES):
        sl = slice(4 * c, 4 * c + 4)
        ckc = ck[sl]
        ckT = np.ascontiguousarray(ckc.reshape(4, 512, 4, 2, 64).transpose(0, 3, 4, 2, 1)).reshape(4, 128, 4, 512)
        ccT = np.ascontiguousarray(cc[sl].reshape(4, 2, 4, 128).transpose(3, 2, 0, 1))
        maps.append({
            "xT": np.ascontiguousarray(xp[c, :SP].T),
            "pT": np.ascontiguousarray(pp[c, :SP].T),
            "xsT": np.ascontiguousarray(xs[sl].reshape(TS, D).T),
            "psT": np.ascontiguousarray(psm[sl].reshape(TS, 256).T),
            "wall": wall, "gv": gv, "cw": cw, "biasp": bp, "biass": bs, "biasn": bn, "ident": ident,
            "ckT": ckT, "cv": np.ascontiguousarray(cv[sl].reshape(4, 512, 512)), "ccT": ccT,
        })
    return maps


def _run(inp, NT=16, with_sample=True, trace=False, stop=None):
    key = (NT, with_sample, stop)
    if key not in _NC_CACHE:
        _NC_CACHE[key] = build(NT, with_sample, stop)
    nc = _NC_CACHE[key]
    maps = _prep_inputs(inp, NT)
    res = run_bass_kernel_spmd(nc, maps, core_ids=list(range(NCORES)), trace=trace)
    R = res.results
    SP = NT * T
    y_p = np.stack([R[c]["yT"].T for c in range(NCORES)])
    y_s = np.concatenate([R[c]["ysT"].T.reshape(4, 16, D) for c in range(NCORES)])
    k_p = np.stack([R[c]["kpT"].T.reshape(512, 8, 64) for c in range(NCORES)])[None]
    v_p = np.stack([R[c]["vp"].reshape(512, 8, 64) for c in range(NCORES)])[None]
    c_p = np.stack([R[c]["cpT"].T for c in range(NCORES)])[None]
    k_s = np.concatenate([R[c]["ksT"].T.reshape(4, 16, 8, 64) for c in range(NCORES)])[None]
    v_s = np.concatenate([R[c]["vs"].reshape(4, 16, 8, 64) for c in range(NCORES)])[None]
    c_s = np.concatenate([R[c]["csT"].transpose(1, 2, 0) for c in range(NCORES)])[None]
    outs = (y_p, y_s, k_p, v_p, c_p, k_s, v_s, c_s)
    return tuple(np.ascontiguousarray(o, dtype=np.float32) for o in outs), res


def kernel(**inputs):
    outs, _ = _run(inputs, NT=SEQ // T, with_sample=True)
    return outs
```

```python
import numpy as np
import concourse.bass as bass
import concourse.mybir as mybir
from concourse.bass_utils import run_bass_kernel_spmd
from contextlib import ExitStack

F32 = mybir.dt.float32
BF16 = mybir.dt.bfloat16
AF = mybir.ActivationFunctionType
ALU = mybir.AluOpType

NCORES = 8
D = 1024
DC = 8
DFF = 2816
FC = 22
T = 512
TS = 64
SEQ = 8192
NCV = 8
NSLOT = 10
SLAB = 16
EPS = 1e-6
NEG = -30000.0


def _block_list():
    bl = []

    def ffn(tag):
        for j in range(FC):
            for kc in range(DC):
                bl.append((tag + "_gate", kc, j))
            for kc in range(DC):
                bl.append((tag + "_up", kc, j))
        for m in range(DC):
            for j in range(FC):
                bl.append((tag + "_down", j, m))

    ffn("w1")
    for m in range(8):
        for kc in range(DC):
            bl.append(("w_in", kc, m))
    for kc in range(DC):
        for m in range(8, 12):
            bl.append(("w_in", kc, m))
    for m in range(12, 24):
        for kc in range(DC):
            bl.append(("w_in", kc, m))
    for m in range(DC):
        for kc in range(4):
            bl.append(("w_att_out", kc, m))
        for kc in range(4):
            bl.append(("w_conv_out", kc, m))
        for kc in range(DC):
            bl.append(("w_in", kc, 24 + m))
        for kc in range(DC):
            bl.append(("w_in", kc, 32 + m))
    for m in range(DC):
        for kc in range(DC):
            bl.append(("w_out", kc, m))
    ffn("w2")
    for m in range(DC):
        for kc in range(DC):
            bl.append(("w_ple_gate", kc, m))
        for kc in range(2):
            bl.append(("w_ple_proj", kc, m))
    return bl


BLOCKS = _block_list()
NBLK = len(BLOCKS)
assert NBLK % SLAB == 0
NSLAB = NBLK // SLAB


class Tracker:
    CE = ("pe", "act", "dve", "pool")

    def __init__(self):
        self.prog = {e: [] for e in ("pe", "act", "dve", "pool", "sp")}
        self.cnt = {e: 0 for e in self.CE}
        self.dangling = {e: False for e in self.CE}
        self.known = {e: {} for e in self.prog}
        self.last_w = {}
        self.readers = {}
        self.dcnt = {}

    def _deps(self, eng, reads, writes):
        need = {}

        def add(tok):
            if tok is None:
                return
            s, v = tok
            if need.get(s, 0) < v:
                need[s] = v

        for r in reads:
            add(self.last_w.get(r))
            if isinstance(r, tuple) and r[0] == "ps":
                for s, v in self.readers.get(r, {}).items():
                    if s != eng:
                        add((s, v))
        for w in writes:
            add(self.last_w.get(w))
            for s, v in self.readers.get(w, {}).items():
                add((s, v))
        kn = self.known[eng]
        for s, v in need.items():
            if eng == "pe" and s == "pe":
                continue
            if kn.get(s, 0) >= v:
                continue
            kn[s] = v
            self.prog[eng].append(("wait", s, v))

    def _commit(self, tok, reads, writes):
        for r in reads:
            d = self.readers.setdefault(r, {})
            if d.get(tok[0], 0) < tok[1]:
                d[tok[0]] = tok[1]
        for w in writes:
            self.last_w[w] = tok
            self.readers[w] = {}

    def op(self, eng, fn, reads=(), writes=(), inc=True):
        self._deps(eng, reads, writes)
        if inc:
            self.cnt[eng] += 1
            tok = (eng, self.cnt[eng])
            self.dangling[eng] = False
            self.prog[eng].append(("op", fn, eng, 1))
        else:
            tok = (eng, self.cnt[eng] + 1)
            self.dangling[eng] = True
            self.prog[eng].append(("op", fn, None, 0))
        self._commit(tok, reads, writes)

    def dma(self, queue, fn, sem, reads=(), writes=(), n=1):
        self._deps(queue, reads, writes)
        self.dcnt[sem] = self.dcnt.get(sem, 0) + 16 * n
        tok = (sem, self.dcnt[sem])
        self.prog[queue].append(("dma", fn, sem, n))
        self._commit(tok, reads, writes)

    def final_wait(self, queue, sems):
        for s in sems:
            if s in self.dcnt:
                self.prog[queue].append(("wait", s, self.dcnt[s]))


class _Stop(Exception):
    pass


def build(NT=16, with_sample=True, stop=None):
    nc = bass.Bass("TRN2", target_bir_lowering=False)
    SP = NT * T
    HW = T // 2
    xT = nc.dram_tensor("xT", [D, SP], F32, kind="ExternalInput").ap()
    pT = nc.dram_tensor("pT", [256, SP], F32, kind="ExternalInput").ap()
    xsT = nc.dram_tensor("xsT", [D, TS], F32, kind="ExternalInput").ap()
    psT = nc.dram_tensor("psT", [256, TS], F32, kind="ExternalInput").ap()
    wall = nc.dram_tensor("wall", [NSLAB, 128, SLAB * 128], F32, kind="ExternalInput").ap()
    wbf = nc.dram_tensor("wbf", [NSLAB, 128, SLAB * 128], BF16, kind="Internal").ap()
    gvd = nc.dram_tensor("gv", [128, 56], F32, kind="ExternalInput").ap()
    cwd = nc.dram_tensor("cw", [128, 12], F32, kind="ExternalInput").ap()
    biasd = nc.dram_tensor("biasp", [128, 5, 1024], F32, kind="ExternalInput").ap()
    biassd = nc.dram_tensor("biass", [128, 4, 128], F32, kind="ExternalInput").ap()
    biasnd = nc.dram_tensor("biasn", [16, 128], F32, kind="ExternalInput").ap()
    identd = nc.dram_tensor("ident", [128, 128], F32, kind="ExternalInput").ap()
    ckTd = nc.dram_tensor("ckT", [4, 128, 4, 512], F32, kind="ExternalInput").ap()
    cvd = nc.dram_tensor("cv", [4, 512, 512], F32, kind="ExternalInput").ap()
    ccTd = nc.dram_tensor("ccT", [128, 4, 4, 2], F32, kind="ExternalInput").ap()

    yT = nc.dram_tensor("yT", [D, SP], F32, kind="ExternalOutput").ap()
    ysT = nc.dram_tensor("ysT", [D, TS], F32, kind="ExternalOutput").ap()
    kpT = nc.dram_tensor("kpT", [512, 512], F32, kind="ExternalOutput").ap()
    vp = nc.dram_tensor("vp", [512, 512], F32, kind="ExternalOutput").ap()
    cpT = nc.dram_tensor("cpT", [512, 2], F32, kind="ExternalOutput").ap()
    ksT = nc.dram_tensor("ksT", [512, TS], F32, kind="ExternalOutput").ap()
    vs = nc.dram_tensor("vs", [4, 16, 512], F32, kind="ExternalOutput").ap()
    csT = nc.dram_tensor("csT", [512, 4, 2], F32, kind="ExternalOutput").ap()

    tr = Tracker()
    es = ExitStack()
    with es:
        def sb(name, shape, dt):
            return es.enter_context(nc.sbuf_tensor(name, shape, dt))

        hbuf = [sb("h0", [128, DC, T], F32), sb("h1", [128, DC, T], F32)]
        ub = sb("ub", [128, DC, T], BF16)
        mTb = sb("mT", [128, DC, T], BF16)
        sq = sb("sq", [128, DC, T], BF16)
        ysb = sb("ysb", [128, DC, T], F32)
        hid = sb("hid", [128, FC, T], BF16)
        NTB = 10
        tb = sb("tbuf", [128, NTB, HW], F32)
        rstd = sb("rstd", [128, T], F32)
        qT = sb("qT", [128, 4, T], BF16)
        kT = sb("kT", [128, 4, 2 * T], BF16)
        Vr = sb("Vr", [128, 8, 512], BF16)
        sS = sb("sS", [128, 4, T + 2], F32)
        sSs = sb("sSs", [128, 4, 4, 18], F32)
        ycT = sb("ycT", [128, 4, T], BF16)
        PT = sb("PT", [128, 2, 1024], BF16)
        PTs = sb("PTs", [128, 5, 128], BF16)
        Rr = sb("Rr", [128, 512], F32)
        oT = sb("oT", [128, 4, T], BF16)
        pTb = [sb("pTb0", [128, 2, T], BF16), sb("pTb1", [128, 2, T], BF16)]
        biasT = sb("biasT", [128, 5, 1024], BF16)
        biasS = sb("biasS", [128, 4, 128], BF16)
        biasN = sb("biasN", [16, 128], BF16)
        ident = sb("identb", [128, 128], BF16)
        onesn = sb("onesn", [128, 128], BF16)
        ones1 = sb("ones1", [128, 64], BF16)
        epsb = sb("epsb", [128, 1], F32)
        gv = sb("gvs", [128, 56], F32)
        gh = sb("ghs", [128, 56], F32)
        cw = sb("cws", [128, 12], F32)
        vnew = PT[0:16, :, :].rearrange("p a (b c) -> p (a b) c", c=512)
        wr = sb("wring", [128, NSLOT, SLAB * 128], BF16)
        ps = es.enter_context(nc.psum_tensor("ps", [128, 8, 512], F32))

        OUT_SEMS = ["oh0", "oh1", "o_cp", "o_cs"] + ["ot%d" % i for i in range(NTB)]
        sem_names = ["pe", "act", "dve", "pool", "hs0", "hs1", "ps0", "ps1", "cst", "cst2", "cinit", "ch0", "ch1"] + \
                    OUT_SEMS + ["w%d" % i for i in range(NSLOT)] + ["wb%d" % i for i in range(NSLOT)] + ["wq%d" % i for i in range(NSLOT)]
        sems = {n: es.enter_context(nc.semaphore("s_" + n)) for n in sem_names}

        bank_rot = [0, 1, 2, 3, 4, 7]
        st = {"bank": 0, "tb": 0, "slab_loaded": 0}
        n_pass = NT + (1 if with_sample else 0)
        total_slabs = n_pass * NSLAB

        def nbank():
            b = bank_rot[st["bank"] % len(bank_rot)]
            st["bank"] += 1
            return b

        def ntb():
            i = st["tb"] % NTB
            st["tb"] += 1
            return i

        def load_slab(g):
            slot = g % NSLOT
            dst = wr[:, slot, :]
            if g < NSLAB:
                src = wall[g]
                tr.dma("pool", lambda e, dst=dst, src=src: [e.dma_start(out=dst, in_=src)],
                       "wq%d" % slot, reads=(), writes=(("w", slot),))
                tr.dma("sp", lambda e, dst=dst, g=g: [e.dma_start(out=wbf[g], in_=dst)],
                       "wb%d" % slot, reads=(("w", slot),), writes=(("wbf", g),))
            else:
                src = wbf[g % NSLAB]
                tr.dma("sp", lambda e, dst=dst, src=src: [e.dma_start(out=dst, in_=src)],
                       "w%d" % slot, reads=(("wbf", g % NSLAB),), writes=(("w", slot),))

        def convert_weights():
            for g in range(NSLAB):
                rd = (("wbf", g - NCV),) if g >= NCV else ()
                tr.dma("pool", lambda e, g=g: [e.dma_start(out=wbf[g], in_=wall[g])],
                       "cv%d" % (g % NCV), reads=rd, writes=(("wbf", g),))

        def ensure_slabs(upto):
            while st["slab_loaded"] <= upto and st["slab_loaded"] < total_slabs:
                load_slab(st["slab_loaded"])
                st["slab_loaded"] += 1

        def wget(gb, expect=None):
            if expect is not None:
                assert BLOCKS[gb % NBLK] == expect, (BLOCKS[gb % NBLK], expect)
            g = gb // SLAB
            assert g < st["slab_loaded"], ("slab not issued", g, st["slab_loaded"])
            slot = g % NSLOT
            off = (gb % SLAB) * 128
            return wr[:, slot, off:off + 128], ("w", slot)

        def wget4(gb, expect0):
            assert gb % 4 == 0 and BLOCKS[gb % NBLK] == expect0
            g = gb // SLAB
            assert g < st["slab_loaded"]
            slot = g % NSLOT
            off = (gb % SLAB) * 128
            return wr[:, slot, off:off + 512], ("w", slot)

        def mm_group(out_ap, out_res, terms):
            n = len(terms)
            for i, (l, lr, r, rr) in enumerate(terms):
                tr.op("pe",
                      lambda e, l=l, r=r, i=i: e.matmul(out_ap, lhsT=l, rhs=r, start=(i == 0),
                                                        stop=(i == n - 1)),
                      reads=(lr, rr), writes=(out_res,), inc=(i == n - 1))

        def act(out, in_, func, reads, writes, scale=1.0, bias=None):
            if bias is None:
                tr.op("act", lambda e: e.activation(out=out, in_=in_, func=func, scale=scale),
                      reads=reads, writes=writes)
            else:
                tr.op("act", lambda e: e.activation(out=out, in_=in_, func=func, scale=scale,
                                                    bias=bias), reads=reads, writes=writes)

        def dve(fn, reads, writes):
            tr.op("dve", fn, reads=reads, writes=writes)

        def out_dma(fn, reads, sem, n=1):
            tr.dma("pool", fn, sem, reads=reads, writes=(), n=n)

        tr.dma("sp", lambda e: [e.dma_start(out=gv[:], in_=gvd), e.dma_start(out=cw[:], in_=cwd)],
               "cst", writes=("gv", "cw"), n=2)
        tr.dma("pool", lambda e: [e.dma_start(out=biasT[:, i, :], in_=biasd[:, i, :]) for i in range(5)]
               + [e.dma_start(out=biasS[:], in_=biassd), e.dma_start(out=biasN[:], in_=biasnd),
                  e.dma_start(out=ident[:], in_=identd)],
               "cinit", writes=("biasT", "biasS", "biasN", "ident"), n=8)
        dve(lambda e: e.tensor_scalar(out=gh[:], in0=gv[:], scalar1=0.5, scalar2=None, op0=ALU.mult),
            reads=("gv",), writes=("gh",))
        dve(lambda e: e.memset(onesn[:], 1.0 / 1024.0), reads=(), writes=("onesn",))
        dve(lambda e: e.memset(ones1[:], 1.0), reads=(), writes=("ones1",))
        dve(lambda e: e.memset(epsb[:], EPS), reads=(), writes=("epsb",))
        dve(lambda e: e.memset(sS[:, :, 0:2], 0.0), reads=(), writes=("sS",))
        dve(lambda e: e.memset(
            biasT[0:64, 0, :].rearrange("p (g q) -> p g q", q=128)[:, :, 64:128], NEG),
            reads=(), writes=("biasT",))
        dve(lambda e: e.memset(
            biasT[64:128, 4, :].rearrange("p (g q) -> p g q", q=128)[:, :, 0:64], NEG),
            reads=(), writes=("biasT",))

        class Half:
            def __init__(self, i, hf, sample=False):
                self.i, self.hf, self.sample = i, hf, sample
                self.W = TS if sample else HW
                self.c0 = 0 if sample else hf * HW
                self.cs = slice(self.c0, self.c0 + self.W)
                self.h = hbuf[i % 2]
                self.hres = "h%d" % (i % 2)
                self.pb = pTb[i % 2]
                self.pres = "p%d" % (i % 2)
                self.gb0 = i * NBLK
                self.last = (not sample) and (i == NT - 1)

            def R(self, name, *idx):
                return (name, self.hf) + idx

            def H(self, c):
                return (self.hres, self.hf, c)

        def stats_all(hv):
            bn = nbank()
            mm_group(ps[:, bn, 0:hv.W], ("ps", bn),
                     [(onesn[:], "onesn", sq[:, c, hv.cs], hv.R("sq", c)) for c in range(DC)])
            ti = ntb()
            act(tb[:, ti, 0:hv.W], ps[:, bn, 0:hv.W], AF.Ln, reads=(("ps", bn), "epsb"),
                writes=(("tb", ti),), bias=epsb[:, 0:1])
            act(rstd[:, hv.cs], tb[:, ti, 0:hv.W], AF.Exp, reads=(("tb", ti),), writes=(hv.R("rstd"),),
                scale=-0.5)

        def u_pre_sq(gi):
            def fn(c):
                def f(hv, gb):
                    act(sq[:, c, hv.cs], hv.h[:, c, hv.cs], AF.Square, reads=(hv.H(c),),
                        writes=(hv.R("sq", c),))
                return f
            return [(0, 0.4, fn(c)) for c in range(DC)]

        def u_op(hs, hv, gi, c, to_m):
            dst, dres = (mTb, "mT") if to_m else (ub, "ub")
            dve(lambda e: e.scalar_tensor_tensor(
                out=dst[:, c, hv.cs], in0=hs.h[:, c, hv.cs], scalar=gv[:, gi * 8 + c:gi * 8 + c + 1],
                in1=rstd[:, hv.cs], op0=ALU.mult, op1=ALU.mult),
                reads=(hs.H(c), "gv", hv.R("rstd")), writes=(hv.R(dres, c),))

        def u_pre_fin(gi, to_m=False):
            def fin(hv, gb):
                stats_all(hv)

            def fn(c):
                def f(hv, gb):
                    u_op(hv, hv, gi, c, to_m)
                return f
            return [(0, 2.3, fin)] + [(0, 0.45, fn(c)) for c in range(DC)]

        def u_post(gi, then_sq, next_sq=False, then_hb=False):
            def first(hv, gb):
                stats_all(hv)

            def fn(c):
                def f(hv, gb):
                    dve(lambda e: e.tensor_tensor(out=ysb[:, c, hv.cs], in0=ysb[:, c, hv.cs],
                                                  in1=rstd[:, hv.cs], op=ALU.mult),
                        reads=(hv.R("ysb", c), hv.R("rstd")), writes=(hv.R("ysb", c),))
                    dve(lambda e: e.tensor_tensor(out=hv.h[:, c, hv.cs], in0=hv.h[:, c, hv.cs],
                                                  in1=ysb[:, c, hv.cs], op=ALU.add),
                        reads=(hv.R("ysb", c), hv.H(c)), writes=(hv.H(c),))
                    if then_sq:
                        act(sq[:, c, hv.cs], hv.h[:, c, hv.cs], AF.Square, reads=(hv.H(c),),
                            writes=(hv.R("sq", c),))
                    if then_hb:
                        act(ub[:, c, hv.cs], hv.h[:, c, hv.cs], AF.Copy, reads=(hv.H(c),),
                            writes=(hv.R("ub", c),))
                    if next_sq and hv.next is not None:
                        hn = hv.next
                        act(sq[:, c, hv.cs], hn.h[:, c, hv.cs], AF.Square, reads=(hn.H(c),),
                            writes=(hv.R("sq", c),))
                return f

            def nfin(hv, gb):
                if hv.next is not None:
                    stats_all(hv)
            us = [(0, 2.5, first)] + [(0, 0.9, fn(c)) for c in range(DC)]
            if next_sq:
                us.append((0, 2.3, nfin))
            return us

        def evac_sq(hv, bn, m, gi, half, doubled=False):
            assert not (half and doubled)
            g_ = gh if (half or doubled) else gv
            act(ysb[:, m, hv.cs], ps[:, bn, 0:hv.W], AF.Identity,
                reads=(("ps", bn), "gh" if (half or doubled) else "gv"), writes=(hv.R("ysb", m),),
                scale=g_[:, gi * 8 + m:gi * 8 + m + 1])
            act(sq[:, m, hv.cs], ps[:, bn, 0:hv.W], AF.Square, reads=(("ps", bn),), writes=(hv.R("sq", m),),
                scale=(0.5 if doubled else 1.0))

        def u_ffn(tag, gpost, from_m=False, with_next=False):
            us = []
            src, sres = (mTb, "mT") if from_m else (ub, "ub")

            def gu(j):
                def f(hv, gb):
                    W = hv.W
                    bn = nbank()
                    terms = []
                    for kc in range(DC):
                        w, wres = wget(gb + kc, (tag + "_gate", kc, j))
                        terms.append((w, wres, src[:, kc, hv.cs], hv.R(sres, kc)))
                    mm_group(ps[:, bn, 0:W], ("ps", bn), terms)
                    terms = []
                    for kc in range(DC):
                        w, wres = wget(gb + 8 + kc, (tag + "_up", kc, j))
                        terms.append((w, wres, src[:, kc, hv.cs], hv.R(sres, kc)))
                    mm_group(ps[:, bn, 256:256 + W], ("ps", bn), terms)
                    ti = ntb()
                    act(tb[:, ti, 0:W], ps[:, bn, 0:W], AF.Silu, reads=(("ps", bn),), writes=(("tb", ti),))
                    dve(lambda e: e.tensor_tensor(out=hid[:, j, hv.cs], in0=tb[:, ti, 0:W],
                                                  in1=ps[:, bn, 256:256 + W], op=ALU.mult),
                        reads=(("tb", ti), ("ps", bn)), writes=(hv.R("hid", j),))
                return f

            def down(m):
                def f(hv, gb):
                    bn = nbank()
                    terms = []
                    for j in range(FC):
                        w, wres = wget(gb + j, (tag + "_down", j, m))
                        terms.append((w, wres, hid[:, j, hv.cs], hv.R("hid", j)))
                    mm_group(ps[:, bn, 0:hv.W], ("ps", bn), terms)
                    evac_sq(hv, bn, m, gpost, True)
                return f
            nxt = []
            if with_next:
                def nsq(c):
                    def f(hv, gb):
                        if hv.next is not None:
                            hn = hv.next
                            act(sq[:, c, hv.cs], hn.h[:, c, hv.cs], AF.Square, reads=(hn.H(c),),
                                writes=(hv.R("sq", c),))
                    return f

                def nfin(hv, gb):
                    if hv.next is not None:
                        stats_all(hv)

                def nun(c):
                    def f(hv, gb):
                        if hv.next is not None:
                            u_op(hv.next, hv, 0, c, True)
                    return f
                nxt = [(0, 0.3, nsq(c)) for c in range(DC)] + [(0, 1.0, nfin)] + \
                      [(0, 0.45, nun(c)) for c in range(DC)]
            for j in range(FC):
                us.append((16, None, gu(j)))
                if j >= 2 and nxt:
                    us.append(nxt.pop(0))
            assert not nxt
            for m in range(DC):
                us.append((22, None, down(m)))
            return us

        def dense8(hv, gb, m):
            bn = nbank()
            terms = []
            for kc in range(DC):
                w, wres = wget(gb + kc, ("w_in", kc, m))
                terms.append((w, wres, ub[:, kc, hv.cs], hv.R("ub", kc)))
            mm_group(ps[:, bn, 0:hv.W], ("ps", bn), terms)
            return bn

        def u_inproj_prompt():
            us = []

            def q(m):
                def f(hv, gb):
                    bn = dense8(hv, gb, m)
                    act(qT[:, m, hv.cs], ps[:, bn, 0:hv.W], AF.Copy, reads=(("ps", bn),),
                        writes=(hv.R("qT", m),), scale=0.125)
                return f

            def k(m):
                def f(hv, gb):
                    bn = dense8(hv, gb, 4 + m)
                    gblk = 4 * hv.i + 2 * hv.hf
                    col = (gblk % 8) * 128
                    act(kT[:, m, col:col + hv.W], ps[:, bn, 0:hv.W], AF.Copy, reads=(("ps", bn),),
                        writes=(("kT", gblk % 8, m), ("kT", (gblk + 1) % 8, m)))
                    if hv.last:
                        ti = ntb()
                        dve(lambda e: e.tensor_copy(out=tb[:, ti, :], in_=ps[:, bn, 0:hv.W]),
                            reads=(("ps", bn),), writes=(("tb", ti),))
                        out_dma(lambda e: [e.dma_start(out=kpT[m * 128:(m + 1) * 128, hv.cs], in_=tb[:, ti, :])],
                                reads=(("tb", ti),), sem="ot%d" % ti)
                return f

            def v(hv, gb):
                wv = [wget4(gb + 4 * kc, ("w_in", kc, 8)) for kc in range(DC)]
                for t2 in range(2):
                    bn = nbank()
                    c0 = hv.c0 + t2 * 128
                    mm_group(ps[:, bn, :], ("ps", bn),
                             [(ub[:, kc, c0:c0 + 128], hv.R("ub", kc), wv[kc][0], wv[kc][1])
                              for kc in range(DC)])
                    slot = (4 * hv.i + 2 * hv.hf + t2) % 8
                    dve(lambda e, bn=bn, slot=slot: e.tensor_copy(out=Vr[:, slot, :], in_=ps[:, bn, :]),
                        reads=(("ps", bn),), writes=(("V", slot),))
                    if hv.last:
                        for hh in range(2):
                            ti = ntb()
                            act(tb[:, ti, :], ps[:, bn, hh * 256:(hh + 1) * 256], AF.Copy, reads=(("ps", bn),),
                                writes=(("tb", ti),))
                            r0 = (2 * hv.hf + t2) * 128
                            out_dma(lambda e, ti=ti, r0=r0, hh=hh: [e.dma_start(
                                out=vp[r0:r0 + 128, hh * 256:(hh + 1) * 256], in_=tb[:, ti, :])],
                                reads=(("tb", ti),), sem="ot%d" % ti)

            def x(m):
                def f(hv, gb):
                    bn = dense8(hv, gb, 12 + m)
                    act(ysb[:, m, hv.cs], ps[:, bn, 0:hv.W], AF.Copy, reads=(("ps", bn),),
                        writes=(hv.R("ysb", m),))
                return f

            def b(m):
                def f(hv, gb):
                    bn = dense8(hv, gb, 16 + m)
                    act(ysb[:, 4 + m, hv.cs], ps[:, bn, 0:hv.W], AF.Copy, reads=(("ps", bn),),
                        writes=(hv.R("ysb", 4 + m),))
                return f

            def c(m):
                def f(hv, gb):
                    bn = dense8(hv, gb, 20 + m)
                    dve(lambda e: e.tensor_tensor(out=sS[:, m, 2 + hv.c0:2 + hv.c0 + hv.W],
                                                  in0=ysb[:, m, hv.cs], in1=ps[:, bn, 0:hv.W], op=ALU.mult),
                        reads=(hv.R("ysb", m), ("ps", bn)), writes=("sS",))
                return f
            for m in range(4):
                us.append((8, None, q(m)))
            for m in range(4):
                us.append((8, None, k(m)))
            us.append((32, 4.5, v))
            for m in range(4):
                us.append((8, None, x(m)))
            for m in range(4):
                us.append((8, None, b(m)))
            for m in range(4):
                us.append((8, None, c(m)))
            return us

        def u_conv_prompt():
            def cv(c):
                def f(hv, gb):
                    ti = ntb()
                    o = hv.c0
                    W = hv.W
                    dve(lambda e: e.tensor_scalar(out=tb[:, ti, :], in0=sS[:, c, o:o + W],
                                                  scalar1=cw[:, c:c + 1], scalar2=None, op0=ALU.mult),
                        reads=("sS", "cw"), writes=(("tb", ti),))
                    dve(lambda e: e.scalar_tensor_tensor(out=tb[:, ti, :], in0=sS[:, c, o + 1:o + 1 + W],
                                                         scalar=cw[:, 4 + c:5 + c], in1=tb[:, ti, :],
                                                         op0=ALU.mult, op1=ALU.add),
                        reads=("sS", "cw", ("tb", ti)), writes=(("tb", ti),))
                    dve(lambda e: e.scalar_tensor_tensor(out=tb[:, ti, :], in0=sS[:, c, o + 2:o + 2 + W],
                                                         scalar=cw[:, 8 + c:9 + c], in1=tb[:, ti, :],
                                                         op0=ALU.mult, op1=ALU.add),
                        reads=("sS", "cw", ("tb", ti)), writes=(("tb", ti),))
                    dve(lambda e: e.tensor_tensor(out=ycT[:, c, hv.cs], in0=tb[:, ti, :],
                                                  in1=ysb[:, 4 + c, hv.cs], op=ALU.mult),
                        reads=(("tb", ti), hv.R("ysb", 4 + c)), writes=(hv.R("ycT", c),))
                return f

            def fin(hv, gb):
                if hv.hf == 1:
                    if hv.last:
                        out_dma(lambda e: [e.dma_start(out=cpT.rearrange("(c p) j -> p c j", p=128),
                                                       in_=sS[:, :, T:T + 2])], reads=("sS",), sem="o_cp")
                    else:
                        dve(lambda e: e.tensor_copy(out=sS[:, :, 0:2], in_=sS[:, :, T:T + 2]),
                            reads=("sS",), writes=("sS",))
            return [(0, 1.7, cv(c)) for c in range(4)] + [(0, 0.2, fin)]

        def u_attn_prompt():
            us = []

            def mk(qi, ii_pos):
                def f(hv, gb):
                    stt = hv.att
                    steps = stt["steps"]
                    sidx = stt["next"]
                    stt["next"] += 1
                    if sidx >= len(steps):
                        return
                    if sidx == 0:
                        emit_scores(hv, 0)
                    if sidx + 1 < len(steps):
                        emit_scores(hv, sidx + 1)
                    emit_pv(hv, sidx)
                return f
            for k_ in range(10):
                us.append((0, 1.9, mk(0, 0), "att_first" if k_ == 0 else ("att_last" if k_ == 9 else "att")))
            return us

        def att_setup(hv):
            steps = []
            for qq in range(2):
                qb = 2 * hv.hf + qq
                G = 4 * hv.i + qb
                iis = [ii for ii in range(5) if G - 4 + ii >= 0]
                for n_, ii in enumerate(iis):
                    steps.append((qb, G, ii, n_ == 0, n_ == len(iis) - 1))
            hv.att = {"steps": steps, "next": 0}

        def emit_scores(hv, sidx):
            qb, G, ii, first, lastk = hv.att["steps"][sidx]
            gk = G - 4 + ii
            kcol = (gk % 8) * 128
            sb0 = 0 if (sidx % 2 == 0) else 2
            pp = sidx % 2
            for par in range(2):
                bn = sb0 + par
                tr.op("pe", lambda e, bn=bn, par=par: e.matmul(
                    ps[:, bn, :], lhsT=ident[:], rhs=biasT[:, ii, par * 512:(par + 1) * 512],
                    start=True, stop=False, skip_group_check=True),
                    reads=("ident", "biasT"), writes=(("ps", bn),), inc=False)
            for hc in range(4):
                for par in range(2):
                    bn = sb0 + par
                    tr.op("pe", lambda e, bn=bn, par=par, hc=hc: e.matmul(
                        ps[:, bn, hc * 128:(hc + 1) * 128],
                        lhsT=kT[par * 64:(par + 1) * 64, hc, kcol:kcol + 128],
                        rhs=qT[par * 64:(par + 1) * 64, hc, qb * 128:(qb + 1) * 128],
                        start=False, stop=True, skip_group_check=True),
                        reads=(("kT", gk % 8, hc), hv.R("qT", hc)), writes=(("ps", bn),),
                        inc=(hc == 3 and par == 1))
            act(PT[:, pp, :].rearrange("p (a b) -> p a b", a=2), ps[:, sb0:sb0 + 2, :], AF.Exp,
                reads=(("ps", sb0), ("ps", sb0 + 1)), writes=(("PT", pp),))

        def emit_pv(hv, sidx):
            qb, G, ii, first, lastk = hv.att["steps"][sidx]
            gk = G - 4 + ii
            vslot = gk % 8
            pp = sidx % 2
            bs_, bo_ = 5, 6
            for par in range(2):
                tr.op("pe", lambda e, par=par: e.matmul(
                    ps[par * 64:(par + 1) * 64, bs_, :], lhsT=ones1[:, 0:64],
                    rhs=PT[:, pp, par * 512:(par + 1) * 512], start=first, stop=lastk),
                    reads=("ones1", ("PT", pp)), writes=(("ps", bs_),), inc=False)
            for h_ in range(8):
                par, hc = h_ % 2, h_ // 2
                tr.op("pe", lambda e, par=par, hc=hc, h_=h_: e.matmul(
                    ps[par * 64:(par + 1) * 64, bo_, hc * 128:(hc + 1) * 128],
                    lhsT=Vr[:, vslot, h_ * 64:(h_ + 1) * 64],
                    rhs=PT[:, pp, par * 512 + hc * 128:par * 512 + (hc + 1) * 128],
                    start=(first and hc == 0), stop=lastk, skip_group_check=True),
                    reads=(("V", vslot), ("PT", pp)), writes=(("ps", bo_),), inc=(h_ == 7))
            if lastk:
                act(Rr[:], ps[:, bs_, :], AF.Ln, reads=(("ps", bs_),), writes=("Rr",))
                act(Rr[:], Rr[:], AF.Exp, reads=("Rr",), writes=("Rr",), scale=-1.0)
                dve(lambda e: e.tensor_tensor(
                    out=oT[:, :, qb * 128:(qb + 1) * 128],
                    in0=ps[:, bo_, :].rearrange("p (c q) -> p c q", q=128),
                    in1=Rr[:].rearrange("p (c q) -> p c q", q=128), op=ALU.mult),
                    reads=(("ps", bo_), "Rr"), writes=tuple(hv.R("oT", c) for c in range(4)))

        def u_gating():
            us = []

            def gate(m):
                def f(hv, gb):
                    W = hv.W
                    b1, b2 = nbank(), nbank()
                    terms = []
                    for kc in range(4):
                        w, wres = wget(gb + kc, ("w_att_out", kc, m))
                        terms.append((w, wres, oT[:, kc, hv.cs], hv.R("oT", kc)))
                    mm_group(ps[:, b1, 0:W], ("ps", b1), terms)
                    terms = []
                    for kc in range(4):
                        w, wres = wget(gb + 4 + kc, ("w_conv_out", kc, m))
                        terms.append((w, wres, ycT[:, kc, hv.cs], hv.R("ycT", kc)))
                    mm_group(ps[:, b1, 256:256 + W], ("ps", b1), terms)
                    terms = []
                    for kc in range(DC):
                        w, wres = wget(gb + 8 + kc, ("w_in", kc, 24 + m))
                        terms.append((w, wres, ub[:, kc, hv.cs], hv.R("ub", kc)))
                    mm_group(ps[:, b2, 0:W], ("ps", b2), terms)
                    terms = []
                    for kc in range(DC):
                        w, wres = wget(gb + 16 + kc, ("w_in", kc, 32 + m))
                        terms.append((w, wres, ub[:, kc, hv.cs], hv.R("ub", kc)))
                    mm_group(ps[:, b2, 256:256 + W], ("ps", b2), terms)
                    t1, t2 = ntb(), ntb()
                    act(tb[:, t1, 0:W], ps[:, b2, 0:W], AF.Tanh, reads=(("ps", b2),), writes=(("tb", t1),),
                        scale=0.5)
                    act(tb[:, t2, 0:W], ps[:, b2, 256:256 + W], AF.Tanh, reads=(("ps", b2),),
                        writes=(("tb", t2),), scale=0.5)
                    dve(lambda e: e.scalar_tensor_tensor(out=tb[:, t1, 0:W], in0=tb[:, t1, 0:W], scalar=1.0,
                                                         in1=ps[:, b1, 0:W], op0=ALU.add, op1=ALU.mult),
                        reads=(("tb", t1), ("ps", b1)), writes=(("tb", t1),))
                    dve(lambda e: e.scalar_tensor_tensor(out=tb[:, t2, 0:W], in0=tb[:, t2, 0:W], scalar=1.0,
                                                         in1=ps[:, b1, 256:256 + W], op0=ALU.add, op1=ALU.mult),
                        reads=(("tb", t2), ("ps", b1)), writes=(("tb", t2),))
                    dve(lambda e: e.tensor_tensor(out=mTb[:, m, hv.cs], in0=tb[:, t1, 0:W],
                                                  in1=tb[:, t2, 0:W], op=ALU.add),
                        reads=(("tb", t1), ("tb", t2)), writes=(hv.R("mT", m),))
                return f

            def outp(m):
                def f(hv, gb):
                    bn = nbank()
                    terms = []
                    for kc in range(DC):
                        w, wres = wget(gb + kc, ("w_out", kc, m))
                        terms.append((w, wres, mTb[:, kc, hv.cs], hv.R("mT", kc)))
                    mm_group(ps[:, bn, 0:hv.W], ("ps", bn), terms)
                    evac_sq(hv, bn, m, 3, False, doubled=True)
                return f
            for m in range(DC):
                us.append((24, None, gate(m)))
            for m in range(DC):
                us.append((8, None, outp(m)))
            return us

        def u_ple():
            us = []

            def hb(c):
                def f(hv, gb):
                    act(ub[:, c, hv.cs], hv.h[:, c, hv.cs], AF.Copy, reads=(hv.H(c),), writes=(hv.R("ub", c),))
                return f

            def pl(m):
                def f(hv, gb):
                    W = hv.W
                    bn = nbank()
                    terms = []
                    for kc in range(DC):
                        w, wres = wget(gb + kc, ("w_ple_gate", kc, m))
                        terms.append((w, wres, ub[:, kc, hv.cs], hv.R("ub", kc)))
                    mm_group(ps[:, bn, 0:W], ("ps", bn), terms)
                    terms = []
                    for kc in range(2):
                        w, wres = wget(gb + 8 + kc, ("w_ple_proj", kc, m))
                        terms.append((w, wres, hv.pb[:, kc, hv.cs], hv.pres))
                    mm_group(ps[:, bn, 256:256 + W], ("ps", bn), terms)
                    ti = ntb()
                    act(tb[:, ti, 0:W], ps[:, bn, 0:W], AF.Tanh, reads=(("ps", bn),), writes=(("tb", ti),),
                        scale=0.5)
                    dve(lambda e: e.scalar_tensor_tensor(out=ysb[:, m, hv.cs], in0=tb[:, ti, 0:W], scalar=1.0,
                                                         in1=ps[:, bn, 256:256 + W], op0=ALU.add, op1=ALU.mult),
                        reads=(("tb", ti), ("ps", bn)), writes=(hv.R("ysb", m),))
                    act(sq[:, m, hv.cs], ysb[:, m, hv.cs], AF.Square, reads=(hv.R("ysb", m),),
                        writes=(hv.R("sq", m),), scale=0.5)
                    act(ysb[:, m, hv.cs], ysb[:, m, hv.cs], AF.Identity, reads=(hv.R("ysb", m), "gh"),
                        writes=(hv.R("ysb", m),), scale=gh[:, 6 * 8 + m:6 * 8 + m + 1])
                return f
            def un(c):
                def f(hv, gb):
                    if hv.next is not None:
                        u_op(hv.next, hv, 0, c, True)
                return f
            for m in range(DC):
                us.append((10, None, pl(m)))
            return us

        def u_output():
            def f(hv, gb):
                hres = hv.hres
                if hv.sample:
                    out_dma(lambda e: [e.dma_start(out=ysT.rearrange("(c p) t -> p c t", p=128),
                                                   in_=hv.h[:, :, 0:TS])],
                            reads=tuple(hv.H(c) for c in range(DC)), sem="oh%d" % (hv.i % 2))
                else:
                    t0 = hv.i * T + hv.c0
                    dst = yT.rearrange("(c p) t -> p c t", p=128)[:, :, t0:t0 + hv.W]
                    out_dma(lambda e: [e.dma_start(out=dst, in_=hv.h[:, :, hv.cs])],
                            reads=tuple(hv.H(c) for c in range(DC)), sem="oh%d" % (hv.i % 2))
            return [(0, 0.1, f, "output")]

        def u_inproj_sample():
            us = []

            def pre(hv, gb):
                load_cache(0)
                load_cache(1)
                tr.dma("sp", lambda e: [e.dma_start(out=sSs[:, :, :, 0:2], in_=ccTd)], "cst2",
                       writes=("sSs",))

            def q(m):
                def f(hv, gb):
                    bn = dense8(hv, gb, m)
                    act(qT[:, m, hv.cs], ps[:, bn, 0:hv.W], AF.Copy, reads=(("ps", bn),),
                        writes=(hv.R("qT", m),), scale=0.125)
                return f

            def k(m):
                def f(hv, gb):
                    bn = dense8(hv, gb, 4 + m)
                    act(mTb[:, m, 0:TS], ps[:, bn, 0:TS], AF.Copy, reads=(("ps", bn),), writes=(hv.R("mT", m),))
                    ti = ntb()
                    dve(lambda e: e.tensor_copy(out=tb[:, ti, 0:TS], in_=ps[:, bn, 0:TS]),
                        reads=(("ps", bn),), writes=(("tb", ti),))
                    out_dma(lambda e: [e.dma_start(out=ksT[m * 128:(m + 1) * 128, :], in_=tb[:, ti, 0:TS])],
                            reads=(("tb", ti),), sem="ot%d" % ti)
                return f

            def v(hv, gb):
                wv = [wget4(gb + 4 * kc, ("w_in", kc, 8)) for kc in range(DC)]
                for b in range(4):
                    bn = nbank()
                    mm_group(ps[0:16, bn, :], ("ps", bn),
                             [(ub[:, kc, b * 16:(b + 1) * 16], hv.R("ub", kc), wv[kc][0], wv[kc][1])
                              for kc in range(DC)])
                    dve(lambda e, b=b, bn=bn: e.tensor_copy(out=vnew[:, b, :], in_=ps[0:16, bn, :]),
                        reads=(("ps", bn),), writes=("vnew",))
                    for hh in range(2):
                        ti = ntb()
                        act(tb[0:16, ti, :], ps[0:16, bn, hh * 256:(hh + 1) * 256], AF.Copy,
                            reads=(("ps", bn),), writes=(("tb", ti),))
                        out_dma(lambda e, b=b, ti=ti, hh=hh: [e.dma_start(
                            out=vs[b, :, hh * 256:(hh + 1) * 256], in_=tb[0:16, ti, :])],
                            reads=(("tb", ti),), sem="ot%d" % ti)

            def x(m):
                def f(hv, gb):
                    bn = dense8(hv, gb, 12 + m)
                    act(ysb[:, m, 0:TS], ps[:, bn, 0:TS], AF.Copy, reads=(("ps", bn),), writes=(hv.R("ysb", m),))
                return f

            def b_(m):
                def f(hv, gb):
                    bn = dense8(hv, gb, 16 + m)
                    act(ysb[:, 4 + m, 0:TS], ps[:, bn, 0:TS], AF.Copy, reads=(("ps", bn),),
                        writes=(hv.R("ysb", 4 + m),))
                return f

            def c(m):
                def f(hv, gb):
                    bn = dense8(hv, gb, 20 + m)
                    dve(lambda e: e.tensor_tensor(
                        out=sSs[:, m, :, 2:18],
                        in0=ysb[:, m, 0:TS].rearrange("p (b t) -> p b t", t=16),
                        in1=ps[:, bn, 0:TS].rearrange("p (b t) -> p b t", t=16), op=ALU.mult),
                        reads=(hv.R("ysb", m), ("ps", bn)), writes=("sSs",))
                return f
            us.append((0, 0.1, pre))
            for m in range(4):
                us.append((8, None, q(m)))
            for m in range(4):
                us.append((8, None, k(m)))
            us.append((32, 4.5, v))
            for m in range(4):
                us.append((8, None, x(m)))
            for m in range(4):
                us.append((8, None, b_(m)))
            for m in range(4):
                us.append((8, None, c(m)))
            return us

        def load_cache(b):
            hf = b % 2
            tr.dma("pool", lambda e, b=b, hf=hf: [
                e.dma_start(out=kT[:, :, hf * 512:(hf + 1) * 512], in_=ckTd[b]),
                e.dma_start(out=Vr[:, hf * 4:(hf + 1) * 4, :],
                            in_=cvd[b].rearrange("(k p) f -> p k f", p=128))],
                "ch%d" % hf, writes=tuple(("kT", hf * 4 + k4, m) for m in range(4) for k4 in range(4)) +
                tuple(("V", hf * 4 + k) for k in range(4)), n=2)

        def u_attn_sample():
            def seq(b):
                def f(hv, gb):
                    kTs = mTb
                    hf = b % 2
                    bs_, bo_ = nbank(), nbank()
                    for blk in range(5):
                        KP = 128 if blk < 4 else 16
                        for par in range(2):
                            bn = nbank()
                            if blk < 4:
                                tr.op("pe", lambda e, bn=bn, blk=blk, par=par: e.matmul(
                                    ps[:, bn, 0:64], lhsT=ident[:], rhs=biasS[:, blk, par * 64:(par + 1) * 64],
                                    start=True, stop=False, skip_group_check=True),
                                    reads=("ident", "biasS"), writes=(("ps", bn),), inc=False)
                            else:
                                tr.op("pe", lambda e, bn=bn, par=par: e.matmul(
                                    ps[0:16, bn, 0:64], lhsT=ident[0:16, 0:16],
                                    rhs=biasN[:, par * 64:(par + 1) * 64], start=True, stop=False,
                                    skip_group_check=True),
                                    reads=("ident", "biasN"), writes=(("ps", bn),), inc=False)
                            for hc in range(4):
                                if blk < 4:
                                    l = kT[par * 64:(par + 1) * 64, hc,
                                           hf * 512 + blk * 128:hf * 512 + (blk + 1) * 128]
                                    lres = ("kT", hf * 4 + blk, hc)
                                else:
                                    l = kTs[par * 64:(par + 1) * 64, hc, b * 16:(b + 1) * 16]
                                    lres = hv.R("mT", hc)
                                tr.op("pe", lambda e, bn=bn, l=l, par=par, hc=hc, KP=KP: e.matmul(
                                    ps[0:KP, bn, hc * 16:(hc + 1) * 16], lhsT=l,
                                    rhs=qT[par * 64:(par + 1) * 64, hc, b * 16:(b + 1) * 16],
                                    start=False, stop=True, skip_group_check=True),
                                    reads=(lres, hv.R("qT", hc)), writes=(("ps", bn),), inc=(hc == 3))
                            act(PTs[0:KP, blk, par * 64:(par + 1) * 64], ps[0:KP, bn, 0:64], AF.Exp,
                                reads=(("ps", bn),), writes=(("PTs", blk),))
                    for blk in range(5):
                        KP = 128 if blk < 4 else 16
                        for par in range(2):
                            tr.op("pe", lambda e, par=par, blk=blk, KP=KP: e.matmul(
                                ps[par * 64:(par + 1) * 64, bs_, 0:64], lhsT=ones1[0:KP, 0:64],
                                rhs=PTs[0:KP, blk, par * 64:(par + 1) * 64], start=(blk == 0), stop=(blk == 4)),
                                reads=("ones1", ("PTs", blk)), writes=(("ps", bs_),), inc=False)
                        for h_ in range(8):
                            par, hc = h_ % 2, h_ // 2
                            if blk < 4:
                                l = Vr[:, hf * 4 + blk, h_ * 64:(h_ + 1) * 64]
                                lres = ("V", hf * 4 + blk)
                            else:
                                l = vnew[0:16, b, h_ * 64:(h_ + 1) * 64]
                                lres = "vnew"
                            tr.op("pe", lambda e, l=l, par=par, hc=hc, blk=blk, KP=KP: e.matmul(
                                ps[par * 64:(par + 1) * 64, bo_, hc * 16:(hc + 1) * 16], lhsT=l,
                                rhs=PTs[0:KP, blk, par * 64 + hc * 16:par * 64 + (hc + 1) * 16],
                                start=(blk == 0 and hc == 0), stop=(blk == 4), skip_group_check=True),
                                reads=(lres, ("PTs", blk)), writes=(("ps", bo_),), inc=(h_ == 7))
                    act(Rr[:, 0:64], ps[:, bs_, 0:64], AF.Ln, reads=(("ps", bs_),), writes=("Rr",))
                    act(Rr[:, 0:64], Rr[:, 0:64], AF.Exp, reads=("Rr",), writes=("Rr",), scale=-1.0)
                    dve(lambda e: e.tensor_tensor(
                        out=oT[:, :, b * 16:(b + 1) * 16],
                        in0=ps[:, bo_, 0:64].rearrange("p (c q) -> p c q", q=16),
                        in1=Rr[:, 0:64].rearrange("p (c q) -> p c q", q=16), op=ALU.mult),
                        reads=(("ps", bo_), "Rr"), writes=tuple(hv.R("oT", c) for c in range(4)))
                    if b + 2 < 4:
                        load_cache(b + 2)
                return f

            def cv(c):
                def f(hv, gb):
                    ti = ntb()
                    tv = tb[:, ti, 0:TS].rearrange("p (b t) -> p b t", t=16)
                    dve(lambda e: e.tensor_scalar(out=tv, in0=sSs[:, c, :, 0:16], scalar1=cw[:, c:c + 1],
                                                  scalar2=None, op0=ALU.mult),
                        reads=("sSs", "cw"), writes=(("tb", ti),))
                    dve(lambda e: e.scalar_tensor_tensor(out=tv, in0=sSs[:, c, :, 1:17],
                                                         scalar=cw[:, 4 + c:5 + c], in1=tv,
                                                         op0=ALU.mult, op1=ALU.add),
                        reads=("sSs", "cw", ("tb", ti)), writes=(("tb", ti),))
                    dve(lambda e: e.scalar_tensor_tensor(out=tv, in0=sSs[:, c, :, 2:18],
                                                         scalar=cw[:, 8 + c:9 + c], in1=tv,
                                                         op0=ALU.mult, op1=ALU.add),
                        reads=("sSs", "cw", ("tb", ti)), writes=(("tb", ti),))
                    dve(lambda e: e.tensor_tensor(out=ycT[:, c, 0:TS], in0=tb[:, ti, 0:TS],
                                                  in1=ysb[:, 4 + c, 0:TS], op=ALU.mult),
                        reads=(("tb", ti), hv.R("ysb", 4 + c)), writes=(hv.R("ycT", c),))
                return f

            def fin(hv, gb):
                out_dma(lambda e: [e.dma_start(out=csT[c * 128:(c + 1) * 128], in_=sSs[:, c, :, 16:18])
                                   for c in range(4)], reads=("sSs",), sem="o_cs", n=4)
            return [(0, 3.0, seq(b)) for b in range(4)] + [(0, 0.5, cv(c)) for c in range(4)] + [(0, 0.1, fin)]

        def pass_units(sample):
            us = []
            us += u_ffn("w1", 1, from_m=True)
            us += u_post(1, True) + u_pre_fin(2)
            if sample:
                us += u_inproj_sample() + u_attn_sample()
            else:
                us += u_inproj_prompt() + u_conv_prompt() + u_attn_prompt()
            us += u_gating()
            us += u_post(3, True) + u_pre_fin(4)
            us += u_ffn("w2", 5, with_next=True)
            us += u_post(5, False, then_hb=True)
            us += u_ple()
            us += u_post(6, False)
            us += u_output()
            return us

        PROLOGUE = u_pre_sq(0) + u_pre_fin(0, to_m=True)
        PU = pass_units(False)
        PUS = pass_units(True)
        assert sum(u[0] for u in PU) == NBLK and sum(u[0] for u in PUS) == NBLK, \
            (sum(u[0] for u in PU), sum(u[0] for u in PUS), NBLK)

        def ucost(u):
            return u[1] if u[1] is not None else u[0] * 0.135

        def load_tile(i):
            hb = hbuf[i % 2]
            hres = "h%d" % (i % 2)
            src = xT.rearrange("(c p) t -> p c t", p=128)[:, :, i * T:(i + 1) * T]
            tr.dma("pool", lambda e: [e.dma_start(out=hb[:], in_=src)], "hs%d" % (i % 2),
                   writes=tuple((hres, hf, c) for c in range(DC) for hf in range(2)))
            psrc = pT.rearrange("(c p) t -> p c t", p=128)[:, :, i * T:(i + 1) * T]
            pb = pTb[i % 2]
            tr.dma("pool", lambda e: [e.dma_start(out=pb[:], in_=psrc)], "ps%d" % (i % 2),
                   writes=("p%d" % (i % 2),))

        def load_sample(i):
            hb = hbuf[i % 2]
            hres = "h%d" % (i % 2)
            src = xsT.rearrange("(c p) t -> p c t", p=128)
            tr.dma("pool", lambda e: [e.dma_start(out=hb[:, :, 0:TS], in_=src)], "hs%d" % (i % 2),
                   writes=tuple((hres, hf, c) for c in range(DC) for hf in range(2)))
            psrc = psT.rearrange("(c p) t -> p c t", p=128)
            pb = pTb[i % 2]
            tr.dma("pool", lambda e: [e.dma_start(out=pb[:, :, 0:TS], in_=psrc)], "ps%d" % (i % 2),
                   writes=("p%d" % (i % 2),))

        NTAIL = 10

        def make_stream(hf):
            sl = []
            hvs = [Half(i, hf) for i in range(NT)]
            for i in range(NT):
                hvs[i].next = hvs[i + 1] if i + 1 < NT else None
            pending = []
            for i in range(NT):
                hv = hvs[i]
                gb = hv.gb0
                if i == 0:
                    for u in PROLOGUE:
                        sl.append((hv, u, gb))
                main, tailu = PU[:-NTAIL], PU[-NTAIL:]
                for k_, u in enumerate(main):
                    sl.append((hv, u, gb))
                    gb += u[0]
                    if pending and u[0] > 0:
                        sl.append(pending.pop(0))
                assert not pending
                assert all(u[0] == 0 for u in tailu)
                pending = [(hv, u, gb) for u in tailu]
            sl += pending
            return sl
        SA, SB = make_stream(0), make_stream(1)
        if with_sample:
            hvS = Half(NT, 0, sample=True)
            gb = hvS.gb0
            hvS.next = None
            SS = [(hvS, u, gb) for u in PROLOGUE]
            for u in PUS:
                SS.append((hvS, u, gb))
                gb += u[0]
        else:
            SS = []

        def load_index(i):
            if i < NT:
                load_tile(i)
            elif i == NT and with_sample:
                load_sample(i)

        load_index(0)
        load_index(1)

        def emit(entry):
            hv, u, gb = entry
            if not hv.sample and not hasattr(hv, "att"):
                att_setup(hv)
            u[2](hv, gb)
            if len(u) > 3 and u[3] == "output" and not hv.sample and hv.hf == 1:
                load_index(hv.i + 2)

        ia = ib = 0
        ta = tb_ = 0.0
        LEAD = 12.0
        in_att = [None]

        def kind(ent):
            return ent[1][3] if len(ent[1]) > 3 else ""

        def note(which, ent):
            k = kind(ent)
            if k == "att_first":
                in_att[0] = which
            elif k == "att_last":
                in_att[0] = None
        while ia < len(SA) or ib < len(SB):
            gB = (SB[ib][2] // SLAB) if ib < len(SB) else (total_slabs if not SS else NT * NSLAB)
            ensure_slabs(gB + NSLOT - 1)
            pickA = False
            if ia < len(SA):
                ua = SA[ia]
                lastslab = (ua[2] + max(ua[1][0], 1) - 1) // SLAB
                fits = lastslab <= gB + NSLOT - 1
                if ib >= len(SB):
                    assert fits
                    pickA = True
                elif fits and (ta - tb_) < LEAD and lastslab <= gB + NSLOT - 4:
                    pickA = True
            if in_att[0] == "A":
                pickA = True
            elif in_att[0] == "B":
                pickA = False
            if pickA:
                note("A", SA[ia])
                emit(SA[ia])
                ta += ucost(SA[ia][1])
                ia += 1
            else:
                note("B", SB[ib])
                emit(SB[ib])
                tb_ += ucost(SB[ib][1])
                ib += 1
        for k_, ent in enumerate(SS):
            gS = ent[2] // SLAB
            ensure_slabs(gS + NSLOT - 1)
            emit(ent)
        tr.final_wait("sp", OUT_SEMS)
        for e_ in tr.CE:
            assert not tr.dangling[e_], e_

        def replay(name, eng):
            for item in tr.prog[name]:
                if item[0] == "wait":
                    eng.wait_ge(sems[item[1]], item[2])
                elif item[0] == "op":
                    ins = item[1](eng)
                    if item[2] is not None:
                        ins.then_inc(sems[item[2]], 1)
                else:
                    for ins in item[1](eng):
                        ins.then_inc(sems[item[2]], 16)

        with nc.Block() as block:
            @block.tensor
            def _(e):
                replay("pe", e)

            @block.scalar
            def _(e):
                replay("act", e)

            @block.vector
            def _(e):
                replay("dve", e)

            @block.gpsimd
            def _(e):
                replay("pool", e)

            @block.sync
            def _(e):
                replay("sp", e)
    return nc


def _weights_stream(W):
    wall = np.empty((NSLAB, 128, SLAB * 128), np.float32)
    for b, (name, kc, mc) in enumerate(BLOCKS):
        g, o = divmod(b, SLAB)
        wall[g, :, o * 128:(o + 1) * 128] = W[name][kc * 128:(kc + 1) * 128, mc * 128:(mc + 1) * 128]
    return wall


def _bias_tiles(table):
    kk = np.arange(128)[:, None]
    qq = np.arange(128)[None, :]
    bp = np.empty((128, 5, 1024), np.float32)
    for ii in range(5):
        d = np.clip(128 * (4 - ii) + qq - kk, -128, 128) + 128
        for h in range(8):
            par, hc = h % 2, h // 2
            bp[:, ii, par * 512 + hc * 128:par * 512 + (hc + 1) * 128] = table[h][d]
    tt = np.arange(16)[None, :]
    bs = np.empty((128, 4, 128), np.float32)
    for blk in range(4):
        d = np.clip(512 + tt - (128 * blk + kk), -128, 128) + 128
        for h in range(8):
            par, hc = h % 2, h // 2
            bs[:, blk, par * 64 + hc * 16:par * 64 + (hc + 1) * 16] = table[h][d]
    bn = np.empty((16, 128), np.float32)
    d = np.clip(tt - np.arange(16)[:, None], -128, 128) + 128
    for h in range(8):
        par, hc = h % 2, h // 2
        bn[:, par * 64 + hc * 16:par * 64 + (hc + 1) * 16] = table[h][d]
    return bp, bs, bn


_NC_CACHE = {}


def _prep_inputs(inp, NT):
    f = lambda a: np.asarray(a, dtype=np.float32)
    W = {k: f(inp[k])[0] for k in ("w1_gate", "w1_up", "w1_down", "w_in", "w_att_out", "w_conv_out",
                                   "w_out", "w2_gate", "w2_up", "w2_down", "w_ple_gate", "w_ple_proj")}
    wall = _weights_stream(W)
    g = f(inp["norm_g"])[0]
    gv = np.ascontiguousarray(g.reshape(7, 8, 128).transpose(2, 0, 1).reshape(128, 56))
    cwv = f(inp["conv_w"])[0]
    cw = np.ascontiguousarray(cwv.reshape(3, 4, 128).transpose(2, 0, 1).reshape(128, 12))
    bp, bs, bn = _bias_tiles(f(inp["rel_bias"])[0])
    ident = np.eye(128, dtype=np.float32)
    xp, xs = f(inp["x_prompt"]), f(inp["x_sample"])
    pp, psm = f(inp["p_prompt"])[0], f(inp["p_sample"])[0]
    ck, cv, cc = f(inp["cache_k"])[0], f(inp["cache_v"])[0], f(inp["cache_conv"])[0]
    SP = NT * T
    maps = []
    for c in range(NCORES):
        sl = slice(4 * c, 4 * c + 4)
        ckc = ck[sl]
        ckT = np.ascontiguousarray(ckc.reshape(4, 512, 4, 2, 64).transpose(0, 3, 4, 2, 1)).reshape(4, 128, 4, 512)
        ccT = np.ascontiguousarray(cc[sl].reshape(4, 2, 4, 128).transpose(3, 2, 0, 1))
        maps.append({
            "xT": np.ascontiguousarray(xp[c, :SP].T),
            "pT": np.ascontiguousarray(pp[c, :SP].T),
            "xsT": np.ascontiguousarray(xs[sl].reshape(TS, D).T),
            "psT": np.ascontiguousarray(psm[sl].reshape(TS, 256).T),
            "wall": wall, "gv": gv, "cw": cw, "biasp": bp, "biass": bs, "biasn": bn, "ident": ident,
            "ckT": ckT, "cv": np.ascontiguousarray(cv[sl].reshape(4, 512, 512)), "ccT": ccT,
        })
    return maps


def _run(inp, NT=16, with_sample=True, trace=False, stop=None):
    key = (NT, with_sample, stop)
    if key not in _NC_CACHE:
        _NC_CACHE[key] = build(NT, with_sample, stop)
    nc = _NC_CACHE[key]
    maps = _prep_inputs(inp, NT)
    res = run_bass_kernel_spmd(nc, maps, core_ids=list(range(NCORES)), trace=trace)
    R = res.results
    SP = NT * T
    y_p = np.stack([R[c]["yT"].T for c in range(NCORES)])
    y_s = np.concatenate([R[c]["ysT"].T.reshape(4, 16, D) for c in range(NCORES)])
    k_p = np.stack([R[c]["kpT"].T.reshape(512, 8, 64) for c in range(NCORES)])[None]
    v_p = np.stack([R[c]["vp"].reshape(512, 8, 64) for c in range(NCORES)])[None]
    c_p = np.stack([R[c]["cpT"].T for c in range(NCORES)])[None]
    k_s = np.concatenate([R[c]["ksT"].T.reshape(4, 16, 8, 64) for c in range(NCORES)])[None]
    v_s = np.concatenate([R[c]["vs"].reshape(4, 16, 8, 64) for c in range(NCORES)])[None]
    c_s = np.concatenate([R[c]["csT"].transpose(1, 2, 0) for c in range(NCORES)])[None]
    outs = (y_p, y_s, k_p, v_p, c_p, k_s, v_s, c_s)
    return tuple(np.ascontiguousarray(o, dtype=np.float32) for o in outs), res


def kernel(**inputs):
    outs, _ = _run(inputs, NT=SEQ // T, with_sample=True)
    return outs
```
